# Optimizing a Trainium2 kernel written in Bass

```python
import jax, jax.numpy as jnp
from jax import lax
import numpy as np

D_MODEL = 2048
BATCH = 4
SEQ = 4096
DEPTH = 1

GLA_WIDTH = D_MODEL // 2
ATT_WIDTH = D_MODEL - GLA_WIDTH
MIX_WIDTH = GLA_WIDTH + ATT_WIDTH

GLA_HEADS = 4
GLA_DV = GLA_WIDTH // GLA_HEADS
GLA_DK = GLA_DV // 2
GLA_QK = GLA_HEADS * GLA_DK
GLA_GATE_RANK = 16
GLA_GATE_TAU = 16.0
GLA_CHUNK = 64

ATT_HD = 128
ATT_HEADS = ATT_WIDTH // ATT_HD
DIL_CONFIGS = ((128, 1), (512, 4), (2048, 16))
ATT_BLOCK = 128
ROPE_THETA = 10000.0

D_FF = 5632
FFN_RESIDUAL_WEIGHT = 0.5
EPS = 1e-6

IN_SIZES = (GLA_QK, GLA_QK, GLA_WIDTH, GLA_GATE_RANK, GLA_WIDTH, ATT_WIDTH, ATT_WIDTH, ATT_WIDTH)
D_IN = sum(IN_SIZES)

kernel_name = "hymba_gla_dilated_macaron_layer"


def rms_norm(x, g):
    xf = x.astype(jnp.float32)
    y = xf * lax.rsqrt(jnp.mean(xf * xf, axis=-1, keepdims=True) + EPS)
    return (y * g.astype(jnp.float32)).astype(x.dtype)


def swiglu(x, w_gate, w_up, w_down):
    return (jax.nn.silu(x @ w_gate) * (x @ w_up)) @ w_down


def rope(x, pos):
    half = x.shape[-1] // 2
    inv_freq = 1.0 / (ROPE_THETA ** (jnp.arange(half, dtype=jnp.float32) / half))
    ang = pos.astype(jnp.float32)[:, None] * inv_freq[None, :]
    cos = jnp.cos(ang)[None, :, None, :]
    sin = jnp.sin(ang)[None, :, None, :]
    xf = x.astype(jnp.float32)
    x1, x2 = xf[..., :half], xf[..., half:]
    out = jnp.concatenate([x1 * cos - x2 * sin, x2 * cos + x1 * sin], axis=-1)
    return out.astype(x.dtype)


def gla_chunked(q, k, v, log_a):
    B, S, H, DK = q.shape
    DV = v.shape[-1]
    C = GLA_CHUNK
    n = S // C

    def chunks(t):
        return t.astype(jnp.float32).reshape(B, n, C, H, t.shape[-1]).transpose(1, 0, 3, 2, 4)

    qc = chunks(q) * (DK ** -0.5)
    kc = chunks(k)
    vc = chunks(v)
    b = jnp.cumsum(chunks(log_a), axis=3)
    b_ref = b[:, :, :, C // 2 - 1:C // 2, :]
    b_last = b[:, :, :, C - 1:, :]

    scores = jnp.einsum('nbhid,nbhjd->nbhij', qc * jnp.exp(b - b_ref), kc * jnp.exp(b_ref - b))
    causal = jnp.tril(jnp.ones((C, C), dtype=bool))
    o_intra = jnp.einsum('nbhij,nbhjv->nbhiv', jnp.where(causal, scores, 0.0), vc)

    q_inter = qc * jnp.exp(b)
    k_state = kc * jnp.exp(b_last - b)
    decay = jnp.exp(b_last[:, :, :, 0, :])

    def step(state, xs):
        q_i, k_i, v_i, d_i = xs
        o = jnp.einsum('bhcd,bhdv->bhcv', q_i, state)
        state = d_i[..., None] * state + jnp.einsum('bhcd,bhcv->bhdv', k_i, v_i)
        return state, o

    state0 = jnp.zeros((B, H, DK, DV), jnp.float32)
    _, o_inter = lax.scan(step, state0, (q_inter, k_state, vc, decay))
    o = o_intra + o_inter
    return o.transpose(1, 0, 3, 2, 4).reshape(B, S, H, DV)


def dilated_branch(q, k, v, window, dilation):
    B, S, H, D = q.shape
    r = dilation
    L = S // r
    steps = window // dilation
    blk = ATT_BLOCK
    nb = -(-L // blk)
    Lp = nb * blk

    def to_blocks(t):
        t = t.reshape(B, L, r, H, D).transpose(0, 2, 1, 3, 4)
        t = jnp.pad(t, ((0, 0), (0, 0), (0, Lp - L), (0, 0), (0, 0)))
        return t.reshape(B, r, nb, blk, H, D)

    qb, kb, vb = to_blocks(q), to_blocks(k), to_blocks(v)

    def with_prev(t):
        prev = jnp.concatenate([jnp.zeros_like(t[:, :, :1]), t[:, :, :-1]], axis=2)
        return jnp.concatenate([prev, t], axis=3)

    kk, vv = with_prev(kb), with_prev(vb)
    s = jnp.einsum('brnqhd,brnkhd->brnhqk', qb.astype(jnp.float32), kk.astype(jnp.float32)) * (D ** -0.5)

    qi = jnp.arange(blk)[:, None] + blk
    kj = jnp.arange(2 * blk)[None, :]
    dist = qi - kj
    band = (dist >= 0) & (dist <= steps)
    first = (jnp.arange(nb) == 0)[:, None, None]
    mask = band[None] & ~(first & (kj < blk)[None])
    s = jnp.where(mask[None, None, :, None, :, :], s, -jnp.inf)

    m = jnp.max(s, axis=-1, keepdims=True)
    p = jnp.exp(s - m)
    l = jnp.sum(p, axis=-1, keepdims=True)
    o = jnp.einsum('brnhqk,brnkhd->brnqhd', p / l, vv.astype(jnp.float32))
    lse = (m + jnp.log(l))[..., 0]

    o = o.reshape(B, r, Lp, H, D)[:, :, :L].transpose(0, 2, 1, 3, 4).reshape(B, S, H, D)
    lse = lse.transpose(0, 1, 2, 4, 3).reshape(B, r, Lp, H)[:, :, :L]
    lse = lse.transpose(0, 2, 1, 3).reshape(B, S, H)
    return o, lse


def dilated_attention(q, k, v):
    outs, lses = [], []
    for window, dilation in DIL_CONFIGS:
        o_i, lse_i = dilated_branch(q, k, v, window, dilation)
        outs.append(o_i)
        lses.append(lse_i)
    w = jax.nn.softmax(jnp.stack(lses, axis=0), axis=0)
    return jnp.sum(w[..., None] * jnp.stack(outs, axis=0), axis=0)


def setup_inputs(seed: int = 0) -> dict:
    key = jax.random.key(seed)
    ks = jax.random.split(key, 20)
    f32 = jnp.float32

    def w(k_, shape, fan_in):
        return jax.random.normal(k_, shape, f32) * (fan_in ** -0.5)

    def gain(k_, n):
        return 1.0 + 0.02 * jax.random.normal(k_, (DEPTH, n), f32)

    return {
        "x": jax.random.normal(ks[0], (BATCH, SEQ, D_MODEL), f32),
        "ffn1_norm": gain(ks[1], D_MODEL),
        "ffn1_w_gate": w(ks[2], (DEPTH, D_MODEL, D_FF), D_MODEL),
        "ffn1_w_up": w(ks[3], (DEPTH, D_MODEL, D_FF), D_MODEL),
        "ffn1_w_down": w(ks[4], (DEPTH, D_FF, D_MODEL), D_FF),
        "mix_norm": gain(ks[5], D_MODEL),
        "w_in": w(ks[6], (DEPTH, D_MODEL, D_IN), D_MODEL),
        "gla_gate_up": w(ks[7], (DEPTH, GLA_GATE_RANK, GLA_QK), GLA_GATE_RANK),
        "gla_gate_bias": 0.1 * jax.random.normal(ks[8], (DEPTH, GLA_QK), f32),
        "gla_out_norm": gain(ks[9], GLA_DV),
        "att_q_norm": gain(ks[10], ATT_HD),
        "att_k_norm": gain(ks[11], ATT_HD),
        "w_out": w(ks[12], (DEPTH, MIX_WIDTH, D_MODEL), MIX_WIDTH),
        "ffn2_norm": gain(ks[13], D_MODEL),
        "ffn2_w_gate": w(ks[14], (DEPTH, D_MODEL, D_FF), D_MODEL),
        "ffn2_w_up": w(ks[15], (DEPTH, D_MODEL, D_FF), D_MODEL),
        "ffn2_w_down": w(ks[16], (DEPTH, D_FF, D_MODEL), D_FF),
    }


def reference(x, ffn1_norm, ffn1_w_gate, ffn1_w_up, ffn1_w_down, mix_norm, w_in,
              gla_gate_up, gla_gate_bias, gla_out_norm, att_q_norm, att_k_norm, w_out,
              ffn2_norm, ffn2_w_gate, ffn2_w_up, ffn2_w_down):
    B, S, _ = x.shape
    pos = jnp.arange(S)
    split_points = list(np.cumsum(IN_SIZES)[:-1])
    for l in range(DEPTH):
        x = x + FFN_RESIDUAL_WEIGHT * swiglu(rms_norm(x, ffn1_norm[l]), ffn1_w_gate[l], ffn1_w_up[l], ffn1_w_down[l])

        h = rms_norm(x, mix_norm[l])
        z = h @ w_in[l]
        q_g, k_g, v_g, g_low, r_g, q_a, k_a, v_a = jnp.split(z, split_points, axis=-1)

        gate_logit = (g_low @ gla_gate_up[l] + gla_gate_bias[l]).astype(jnp.float32)
        log_a = jax.nn.log_sigmoid(gate_logit) / GLA_GATE_TAU
        o_g = gla_chunked(q_g.reshape(B, S, GLA_HEADS, GLA_DK),
                          k_g.reshape(B, S, GLA_HEADS, GLA_DK),
                          v_g.reshape(B, S, GLA_HEADS, GLA_DV),
                          log_a.reshape(B, S, GLA_HEADS, GLA_DK))
        o_g = rms_norm(o_g, gla_out_norm[l]).reshape(B, S, GLA_WIDTH)
        o_g = (o_g * jax.nn.silu(r_g.astype(jnp.float32))).astype(x.dtype)

        qa = rope(rms_norm(q_a.reshape(B, S, ATT_HEADS, ATT_HD), att_q_norm[l]), pos)
        ka = rope(rms_norm(k_a.reshape(B, S, ATT_HEADS, ATT_HD), att_k_norm[l]), pos)
        va = v_a.reshape(B, S, ATT_HEADS, ATT_HD)
        o_a = dilated_attention(qa, ka, va).reshape(B, S, ATT_WIDTH).astype(x.dtype)

        x = x + jnp.concatenate([o_g, o_a], axis=-1) @ w_out[l]

        x = x + FFN_RESIDUAL_WEIGHT * swiglu(rms_norm(x, ffn2_norm[l]), ffn2_w_gate[l], ffn2_w_up[l], ffn2_w_down[l])
    return x
```

```python
import numpy as np
import concourse.bass as bass
import concourse.mybir as mybir
from concourse.bass_utils import run_bass_kernel_spmd

F32 = mybir.dt.float32
BF16 = mybir.dt.bfloat16
AF = mybir.ActivationFunctionType
ALU = mybir.AluOpType

D = 2048
DFF = 5632
DIN = 6160
NT = 2048
HP = 1024
EPS = 1e-6
NEG = -30000.0
NR = 8
PHASES = "A0 A1 B C0 C1"
DEBUG_OUT = set()
RGROUPS = [[0, 1], [2, 3], [4, 5], [6, 7]]
C_QG, C_KG, C_VG, C_GL, C_RG, C_QA, C_KA, C_VA = 0, 512, 1024, 2048, 2064, 3088, 4112, 5136


class Prog:
    def __init__(self):
        self.ops = []
        self.bar = None

    def add(self, eng, fn, r=(), w=(), dma=False, kind=None):
        self.ops.append(dict(eng=eng, fn=fn, r=list(r), w=list(w), dma=dma, kind=kind, bar=self.bar,
                             signal=False))
        return len(self.ops) - 1

    def barrier(self, fn):
        i = self.add("dve", fn, kind="barrier")
        self.bar = i

    def schedule(self):
        ops = self.ops
        last_w, readers = {}, {}
        last_eng = {}
        dma_hist = {"sp": [], "pool": []}
        for i, op in enumerate(ops):
            deps = set()
            if op["kind"] == "barrier":
                for e, j in last_eng.items():
                    deps.add(j)
                for e in dma_hist:
                    deps.update(dma_hist[e][-NR:])
            else:
                for k in op["r"]:
                    if k in last_w:
                        deps.add(last_w[k])
                for k in op["w"]:
                    if k in last_w:
                        deps.add(last_w[k])
                    deps.update(readers.get(k, ()))
                if op["bar"] is not None:
                    deps.add(op["bar"])
            deps.discard(i)
            for k in op["r"]:
                readers.setdefault(k, []).append(i)
            for k in op["w"]:
                last_w[k] = i
                readers[k] = []
            dl = []
            for j in deps:
                pj = ops[j]
                if (not pj["dma"]) and (not op["dma"]) and pj["eng"] == "pe" and op["eng"] == "pe" \
                        and pj["kind"] != "cc":
                    continue
                dl.append(j)
            op["deps"] = dl
            if op["dma"]:
                dma_hist[op["eng"]].append(i)
            else:
                last_eng[op["eng"]] = i
        for op in ops:
            for j in op["deps"]:
                if not ops[j]["dma"]:
                    ops[j]["signal"] = True
        cnt = {}
        dcnt = {"sp": 0, "pool": 0}
        ncc = 0
        for op in ops:
            if op["dma"]:
                idx = dcnt[op["eng"]]
                dcnt[op["eng"]] += 1
                op["sem"] = ("ring", op["eng"], idx % NR)
                op["val"] = 16 * (idx // NR + 1)
            elif op["kind"] == "cc":
                op["sem"] = ("cc", ncc)
                op["val"] = 1
                ncc += 1
            elif op["signal"]:
                cnt[op["eng"]] = cnt.get(op["eng"], 0) + 1
                op["sem"] = ("eng", op["eng"])
                op["val"] = cnt[op["eng"]]
        self.ncc = ncc
        for op in ops:
            waits = {}
            for j in op["deps"]:
                pj = ops[j]
                s, v = pj["sem"], pj["val"]
                if waits.get(s, 0) < v:
                    waits[s] = v
            if op["dma"] and op["val"] > 16:
                s = op["sem"]
                waits[s] = max(waits.get(s, 0), op["val"] - 16)
            op["waits"] = waits
        self.final = {}
        for op in ops:
            if op["dma"]:
                self.final[op["sem"]] = op["val"]

    def emit(self, nc):
        ops = self.ops
        sems = {}
        import contextlib
        with contextlib.ExitStack() as st:
            for e in ("pe", "act", "dve"):
                sems[("eng", e)] = st.enter_context(nc.semaphore("s_" + e))
            for e in ("sp", "pool"):
                for k in range(NR):
                    sems[("ring", e, k)] = st.enter_context(nc.semaphore("r_%s%d" % (e, k)))
            for k in range(self.ncc):
                sems[("cc", k)] = st.enter_context(nc.semaphore("cc%d" % k))
            block = st.enter_context(nc.Block())

            def run(name, e):
                waited = {}
                for op in ops:
                    if op["eng"] != name:
                        continue
                    for s, v in op["waits"].items():
                        if waited.get(s, 0) < v:
                            e.wait_ge(sems[s], v)
                            waited[s] = v
                    ins = op["fn"](e)
                    if op["dma"]:
                        ins.then_inc(sems[op["sem"]], 16)
                    elif op["kind"] == "cc":
                        ins.then_inc(sems[op["sem"]], 1)
                    elif op["signal"]:
                        ins.then_inc(sems[op["sem"]], 1)
                if name in ("sp", "pool"):
                    for s, v in self.final.items():
                        if s[1] == name and waited.get(s, 0) < v:
                            e.wait_ge(sems[s], v)

            @block.tensor
            def _(e):
                run("pe", e)

            @block.scalar
            def _(e):
                run("act", e)

            @block.vector
            def _(e):
                run("dve", e)

            @block.gpsimd
            def _(e):
                run("pool", e)

            @block.sync
            def _(e):
                run("sp", e)


class Arena:
    def __init__(self, base_ap, nbytes):
        self.base = base_ap
        self.n = nbytes
        self.off = 0

    def alloc(self, shape, dt):
        assert shape[0] == 128
        n = 1
        for s in shape[1:]:
            n *= s
        nb = n * (4 if dt == F32 else 2)
        nb_al = (nb + 63) // 64 * 64
        assert self.off + nb_al <= self.n, ("arena overflow", self.off, nb_al, self.n)
        ap = self.base[:, self.off // 2:(self.off + nb) // 2]
        self.off += nb_al
        if dt == F32:
            ap = ap.bitcast(F32)
        if len(shape) == 3:
            ap = ap.rearrange("p (a b) -> p a b", a=shape[1])
        elif len(shape) == 4:
            ap = ap.rearrange("p (a b c) -> p a b c", a=shape[1], b=shape[2])
        return ap


def build_nc():
    nc = bass.Bass("TRN2", target_bir_lowering=False)
    P = Prog()

    def din(name, shape, dt=F32):
        return nc.dram_tensor(name, list(shape), dt, kind="ExternalInput").ap()

    def dscr(name, shape, dt):
        if name in DEBUG_OUT:
            return nc.dram_tensor(name, list(shape), dt, kind="ExternalOutput").ap()
        return nc.dram_tensor(name, list(shape), dt).ap()

    x_d = din("x", [NT, D])
    w1g, w1u, w1d = din("w1g", [D, DFF]), din("w1u", [D, DFF]), din("w1d", [DFF, D])
    w2g, w2u, w2d = din("w2g", [D, DFF]), din("w2u", [D, DFF]), din("w2d", [DFF, D])
    win_d, wout_d = din("win", [D, DIN]), din("wout", [D, D])
    n1_d, nm_d, n2_d = din("n1", [1, D]), din("nm", [1, D]), din("n2", [1, D])
    gout_d = din("gout", [1, 256])
    gq_d, gk_d = din("gq", [128, 1]), din("gk", [128, 1])
    gup_d = din("gup", [16, 512])
    gbias_d = din("gbias", [128, 4])
    cos_d, sin_d = din("cosT", [128, NT]), din("sinT", [128, NT])
    m4_d = din("m4", [3, 128, 512])
    mgla_d = din("mgla", [128, 128])
    ident_d = din("ident", [128, 128])
    swap_d = din("swapT", [128, 128])
    flag_d = din("flag", [128, 1])
    out_d = nc.dram_tensor("out", [NT, D], F32, kind="ExternalOutput").ap()

    x1_d = dscr("x1s", [NT, D], F32)
    qgT_d = dscr("qgT", [512, NT], F32)
    kgT_d = dscr("kgT", [512, NT], F32)
    laT_d = dscr("laT", [512, NT], F32)
    vg_d = dscr("vgs", [NT, 1024], BF16)
    rg_d = dscr("rgs", [NT, 1024], BF16)
    qaT_d = dscr("qaT", [1024, NT], BF16)
    kaT_d = [dscr("kaT%d" % i, [512, NT], BF16) for i in range(2)]
    kaT_g = [dscr("kaTg%d" % i, [1024, NT], BF16) for i in range(2)]
    va_d = [dscr("vas%d" % i, [HP, 1024], BF16) for i in range(2)]
    va_g = [dscr("vag%d" % i, [2 * HP, 1024], BF16) for i in range(2)]
    st_d = dscr("sts", [512, 256], F32)
    st_g = dscr("stg", [1024, 256], F32)
    oT_d = dscr("oTs", [D, NT], BF16)

    import contextlib
    with contextlib.ExitStack() as st:
        def sb(name, shape, dt):
            return st.enter_context(nc.sbuf_tensor("sb_" + name, list(shape), dt))

        AB = 172 * 1024
        arena_t = sb("arena", [128, AB // 2], BF16)
        gbc = sb("gbc", [128, D], F32)
        ident = sb("ident", [128, 128], BF16)
        swapT = sb("swapT", [128, 128], BF16)
        ones = sb("ones", [128, 128], BF16)
        mgla = sb("mgla", [128, 128], BF16)
        m4 = sb("m4", [128, 3, 512], BF16)
        gup = sb("gup", [16, 512], F32)
        small = sb("small", [128, 64], F32)
        wgl = sb("wgl", [128, 16, 16], BF16)
        goutb = sb("goutb", [128, 256], F32)
        ps = [st.enter_context(nc.psum_tensor("ps%d" % j, [128, 512], F32)) for j in range(8)]

        negb = small[:, 0:4]
        gq, gk = small[:, 4:5], small[:, 5:6]
        eps_t, one_t, flag = small[:, 6:7], small[:, 7:8], small[:, 8:9]
        ss = [small[:, 10 + j:11 + j] for j in range(2)]
        ss2 = [small[:, 12 + j:13 + j] for j in range(2)]
        rs = [small[:, 14 + j:15 + j] for j in range(2)]
        bar_t = small[:, 16:17]
        gbias_t = small[:, 20:24]

        def barrier():
            P.barrier(lambda e: e.memset(bar_t, 0.0))

        A = Arena(arena_t, AB)
        xs = [A.alloc([128, D], F32) for _ in range(8)]
        hT = A.alloc([128, 16, HP], BF16)
        W256 = [A.alloc([128, 16, 256], BF16) for _ in range(4)]
        tmp_off = A.off
        Wd = [A.alloc([128, 2, D], BF16) for _ in range(2)]
        actT = [A.alloc([128, 2, HP], BF16) for _ in range(2)]
        sg = [A.alloc([128, 512], F32) for _ in range(2)]
        hbs = [A.alloc([128, D], BF16) for _ in range(2)]
        jk = A.alloc([128, D], BF16)
        assert A.off <= AB
        A5 = Arena(arena_t, AB)
        A5.off = tmp_off
        stg = [A5.alloc([128, 1024], BF16) for _ in range(16)]
        cs_sb = A5.alloc([128, 2, HP], F32)

        def xk(i):
            return [("xs", i, d) for d in range(4)]

        def hk(t):
            return [("hT", i) for i in range(4 * t, 4 * t + 4)]

        wcnt = [0]
        pscnt = [0]

        def next_ps():
            j = pscnt[0] % 8
            pscnt[0] += 1
            return j

        P.add("sp", lambda e: e.dma_start(out=gup[:], in_=gup_d), w=[("gup",)], dma=True)
        P.add("sp", lambda e: e.dma_start(out=gbias_t, in_=gbias_d), w=[("small", "gb")], dma=True)
        P.add("sp", lambda e: e.dma_start(out=gq, in_=gq_d), w=[("small", "gq")], dma=True)
        P.add("sp", lambda e: e.dma_start(out=gk, in_=gk_d), w=[("small", "gk")], dma=True)
        P.add("sp", lambda e: e.dma_start(out=flag, in_=flag_d), w=[("small", "flag")], dma=True)
        P.add("sp", lambda e: e.dma_start(out=goutb[:], in_=gout_d.broadcast_to([128, 256])),
              w=[("goutb",)], dma=True)
        P.add("pool", lambda e: e.dma_start(out=ident[:], in_=ident_d), w=[("ident",)], dma=True)
        P.add("pool", lambda e: e.dma_start(out=swapT[:], in_=swap_d), w=[("swapT",)], dma=True)
        P.add("pool", lambda e: e.dma_start(out=mgla[:], in_=mgla_d), w=[("mgla",)], dma=True)
        P.add("pool", lambda e: e.dma_start(out=m4[:], in_=m4_d.rearrange("a p f -> p a f")),
              w=[("m4",)], dma=True)
        P.add("pool", lambda e: e.dma_start(
            out=wgl[:], in_=win_d.rearrange("(k p) f -> p k f", p=128)[:, :, C_GL:C_GL + 16]),
            w=[("wgl",)], dma=True)
        P.add("dve", lambda e: e.memset(ones[:], 1.0), w=[("ones",)])
        P.add("dve", lambda e: e.memset(eps_t, EPS), w=[("small", "eps")])
        P.add("dve", lambda e: e.memset(one_t, 1.0), w=[("small", "one")])
        P.add("dve", lambda e: e.tensor_scalar(out=negb, in0=gbias_t, scalar1=-1.0, scalar2=None,
                                               op0=ALU.mult),
              r=[("small", "gb")], w=[("small", "negb")])

        def load_gain(g_d):
            P.add("sp", lambda e: e.dma_start(out=gbc[:], in_=g_d.broadcast_to([128, D])),
                  w=[("gbc",)], dma=True)

        def norm_stats(i):
            q = i % 2
            hb = hbs[q]
            P.add("dve", lambda e: e.memset(ss[q], 0.0), w=[("ss", q)])
            P.add("act", lambda e: e.activation(out=jk, in_=xs[i], func=AF.Square, accum_out=ss[q]),
                  r=xk(i), w=[("jk",), ("ss", q)])
            P.add("act", lambda e: e.activation(out=ss2[q], in_=ss[q], func=AF.Sqrt, bias=eps_t,
                                                scale=1.0 / D),
                  r=[("ss", q), ("small", "eps")], w=[("ss2", q)])
            P.add("dve", lambda e: e.reciprocal(out=rs[q], in_=ss2[q]), r=[("ss2", q)], w=[("rs", q)])
            P.add("dve", lambda e: e.scalar_tensor_tensor(out=hb, in0=xs[i], scalar=rs[q], in1=gbc[:],
                                                          op0=ALU.mult, op1=ALU.mult),
                  r=xk(i) + [("rs", q), ("gbc",)], w=[("hb", q)])

        def norm_tr(i):
            q = i % 2
            hb = hbs[q]
            for half in range(2):
                j = next_ps()
                pb = ps[j][:].bitcast(BF16)
                for kk in range(8):
                    k = half * 8 + kk
                    P.add("pe", lambda e, k=k, kk=kk, pb=pb: e.transpose(
                        out=pb[:, kk * 128:(kk + 1) * 128], in_=hb[:, k * 128:(k + 1) * 128],
                        identity=ident[:]),
                        r=[("hb", q), ("ident",)], w=[("ps", j)])
                src = pb.rearrange("p (k t) -> p k t", k=8)
                dst = hT[:, half * 8:(half + 1) * 8, i * 128:(i + 1) * 128]
                if half == 0:
                    P.add("act", lambda e, src=src, dst=dst: e.activation(out=dst, in_=src, func=AF.Copy),
                          r=[("ps", j)], w=[("hT", i)])
                else:
                    P.add("dve", lambda e, src=src, dst=dst: e.tensor_copy(out=dst, in_=src),
                          r=[("ps", j)], w=[("hT", i)])

        def norm_all(pre=None):
            for s_ in range(9):
                if s_ < 8:
                    if pre is not None:
                        pre(s_)
                    norm_stats(s_)
                if s_ >= 1:
                    norm_tr(s_ - 1)

        def load_w256(w_d, c0):
            b = wcnt[0] % 4
            wcnt[0] += 1
            P.add("pool", lambda e: e.dma_start(
                out=W256[b], in_=w_d.rearrange("(k p) f -> p k f", p=128)[:, :, c0:c0 + 256]),
                w=[("W", b)], dma=True)
            return b

        def ffn(wg_d, wu_d, wd_d):
            NG = DFF // 256
            st_ = {}

            def gu(g):
                bg = load_w256(wg_d, g * 256)
                bu = load_w256(wu_d, g * 256)
                bd = g % 2
                P.add("pool", lambda e: e.dma_start(
                    out=Wd[bd], in_=wd_d[g * 256:(g + 1) * 256, :].rearrange("(c p) d -> p c d", p=128)),
                    w=[("Wd", bd)], dma=True)
                for c in range(2):
                    for t in range(2):
                        q = (2 * c + t) % 2
                        jg, ju = q, 2 + q
                        for k in range(16):
                            P.add("pe", lambda e, k=k, c=c, t=t, jg=jg: e.matmul(
                                ps[jg][:], lhsT=W256[bg][:, k, c * 128:(c + 1) * 128],
                                rhs=hT[:, k, t * 512:(t + 1) * 512], start=(k == 0), stop=(k == 15)),
                                r=[("W", bg)] + hk(t), w=[("ps", jg)])
                        for k in range(16):
                            P.add("pe", lambda e, k=k, c=c, t=t, ju=ju: e.matmul(
                                ps[ju][:], lhsT=W256[bu][:, k, c * 128:(c + 1) * 128],
                                rhs=hT[:, k, t * 512:(t + 1) * 512], start=(k == 0), stop=(k == 15)),
                                r=[("W", bu)] + hk(t), w=[("ps", ju)])
                        P.add("act", lambda e, jg=jg, q=q: e.activation(out=sg[q], in_=ps[jg][:], func=AF.Silu),
                              r=[("ps", jg)], w=[("sg", q)])
                        P.add("dve", lambda e, ju=ju, q=q, c=c, t=t: e.tensor_tensor(
                            out=actT[bd][:, c, t * 512:(t + 1) * 512], in0=ps[ju][:], in1=sg[q], op=ALU.mult),
                            r=[("ps", ju), ("sg", q)], w=[("actT", bd, c, t)])

            def down(g):
                bd = g % 2
                n = 0
                for i in range(8):
                    for dg in range(4):
                        j = 4 + n % 4
                        n += 1
                        for c in range(2):
                            P.add("pe", lambda e, c=c, i=i, dg=dg, j=j: e.matmul(
                                ps[j][:], lhsT=actT[bd][:, c, i * 128:(i + 1) * 128],
                                rhs=Wd[bd][:, c, dg * 512:(dg + 1) * 512], start=(c == 0), stop=(c == 1)),
                                r=[("actT", bd, c, i // 4), ("Wd", bd)], w=[("ps", j)])
                        P.add("dve", lambda e, i=i, dg=dg, j=j: e.scalar_tensor_tensor(
                            out=xs[i][:, dg * 512:(dg + 1) * 512], in0=ps[j][:], scalar=0.5,
                            in1=xs[i][:, dg * 512:(dg + 1) * 512], op0=ALU.mult, op1=ALU.add),
                            r=[("ps", j), ("xs", i, dg)], w=[("xs", i, dg)])

            gu(0)
            for g in range(1, NG):
                gu(g)
                down(g - 1)
            down(NG - 1)

        def phaseA(hp):
            T0 = hp * HP

            def ld_x(hp_):
                for i in range(8):
                    P.add("sp", lambda e, i=i: e.dma_start(
                        out=xs[i], in_=x_d[hp_ * HP + i * 128:hp_ * HP + (i + 1) * 128, :]),
                        w=xk(i), dma=True)
            if hp == 0:
                ld_x(0)
            load_gain(n1_d)
            norm_all()
            if "noffn" not in PHASES:
                ffn(w1g, w1u, w1d)
            load_gain(nm_d)
            def st_x1(i):
                P.add("sp", lambda e, i=i: e.dma_start(out=x1_d[T0 + i * 128:T0 + (i + 1) * 128, :], in_=xs[i]),
                      r=xk(i), w=[("x1_d", hp, i)], dma=True)
            norm_all(st_x1)
            barrier()
            if hp == 0 and "A1" in PHASES:
                ld_x(1)
            if "nowin" not in PHASES:
                win_phase(hp)
            barrier()

        scnt = [0]

        def next_stg():
            j = scnt[0] % 16
            scnt[0] += 1
            return j

        def win_phase(hp):
            T0 = hp * HP
            P.add("sp", lambda e: e.dma_start(out=cs_sb[:, 0, :], in_=cos_d[:, T0:T0 + HP]), w=[("cos",)], dma=True)
            P.add("sp", lambda e: e.dma_start(out=cs_sb[:, 1, :], in_=sin_d[:, T0:T0 + HP]), w=[("sin",)], dma=True)

            def fm_chunk(b, sub, t, extra_w=()):
                j = next_ps()
                for k in range(16):
                    P.add("pe", lambda e, k=k: e.matmul(
                        ps[j][:], lhsT=W256[b][:, k, sub * 128:(sub + 1) * 128],
                        rhs=hT[:, k, t * 512:(t + 1) * 512], start=(k == 0), stop=(k == 15)),
                        r=[("W", b)] + hk(t), w=[("ps", j)])
                return j

            for (c0, dst_d, nm) in (((C_QG, qgT_d, "qgT"), (C_KG, kgT_d, "kgT")) if "no_wq" not in PHASES else ()):
                for pair in range(2):
                    b = load_w256(win_d, c0 + pair * 256)
                    for sub in range(2):
                        hd = pair * 2 + sub
                        for t in range(2):
                            j = fm_chunk(b, sub, t)
                            s = next_stg()
                            sv = stg[s].bitcast(F32)
                            P.add("act", lambda e, j=j, sv=sv: e.activation(out=sv, in_=ps[j][:], func=AF.Copy),
                                  r=[("ps", j)], w=[("stg", s)])
                            P.add("sp", lambda e, sv=sv, hd=hd, t=t, dst_d=dst_d: e.dma_start(
                                out=dst_d[hd * 128:(hd + 1) * 128, T0 + t * 512:T0 + (t + 1) * 512], in_=sv),
                                r=[("stg", s)], w=[(nm, hd, hp, t)], dma=True)
            for t in (range(2) if "no_wg" not in PHASES else ()):
                j = next_ps()
                for k in range(16):
                    P.add("pe", lambda e, k=k, j=j, t=t: e.matmul(
                        ps[j][0:16, :], lhsT=wgl[:, k, :], rhs=hT[:, k, t * 512:(t + 1) * 512],
                        start=(k == 0), stop=(k == 15)),
                        r=[("wgl",)] + hk(t), w=[("ps", j)])
                s = next_stg()
                gl = stg[s].bitcast(F32)[0:16, :]
                P.add("act", lambda e, j=j, gl=gl: e.activation(out=gl, in_=ps[j][0:16, :], func=AF.Copy),
                      r=[("ps", j)], w=[("stg", s)])
                for hd in range(4):
                    j2 = next_ps()
                    P.add("pe", lambda e, j2=j2, hd=hd, gl=gl: e.matmul(
                        ps[j2][:], lhsT=gup[:, hd * 128:(hd + 1) * 128], rhs=gl, start=True, stop=True),
                        r=[("gup",), ("stg", s)], w=[("ps", j2)])
                    s2 = next_stg()
                    ev = stg[s2].bitcast(F32)
                    P.add("act", lambda e, j2=j2, ev=ev, hd=hd: e.activation(
                        out=ev, in_=ps[j2][:], func=AF.Exp, bias=negb[:, hd:hd + 1], scale=-1.0),
                        r=[("ps", j2), ("small", "negb")], w=[("stg", s2)])
                    P.add("act", lambda e, ev=ev: e.activation(out=ev, in_=ev, func=AF.Ln, bias=one_t, scale=1.0),
                          r=[("stg", s2), ("small", "one")], w=[("stg", s2)])
                    P.add("sp", lambda e, ev=ev, hd=hd, t=t: e.dma_start(
                        out=laT_d[hd * 128:(hd + 1) * 128, T0 + t * 512:T0 + (t + 1) * 512], in_=ev),
                        r=[("stg", s2)], w=[("laT", hd, hp, t)], dma=True)
            for (c0, dst_d, nm, gvec, gkey) in (((C_QA, qaT_d, "qaT", gq, "gq"), (C_KA, kaT_d, "kaT", gk, "gk")) if "no_wa" not in PHASES else ()):
                for pair in range(4):
                    b = load_w256(win_d, c0 + pair * 256)
                    for sub in range(2):
                        hd = pair * 2 + sub
                        for t in range(2):
                            j = fm_chunk(b, sub, t)
                            s_sq, s_xg, s_rs, s_t1, s_t2 = [next_stg() for _ in range(5)]
                            sqb, xg = stg[s_sq][:, 0:512], stg[s_xg][:, 0:512]
                            rsv, t1, t2 = stg[s_rs].bitcast(F32), stg[s_t1].bitcast(F32), stg[s_t2].bitcast(F32)
                            P.add("act", lambda e, j=j, sqb=sqb: e.activation(out=sqb, in_=ps[j][:], func=AF.Square),
                                  r=[("ps", j)], w=[("stg", s_sq)])
                            P.add("act", lambda e, j=j, xg=xg, gvec=gvec: e.activation(
                                out=xg, in_=ps[j][:], func=AF.Identity, scale=gvec),
                                r=[("ps", j), ("small", gkey)], w=[("stg", s_xg)])
                            j1, j2 = next_ps(), next_ps()
                            P.add("pe", lambda e, j1=j1, sqb=sqb: e.matmul(ps[j1][:], lhsT=ones[:], rhs=sqb,
                                                                           start=True, stop=True),
                                  r=[("ones",), ("stg", s_sq)], w=[("ps", j1)])
                            P.add("pe", lambda e, j2=j2, xg=xg: e.matmul(ps[j2][:], lhsT=swapT[:], rhs=xg,
                                                                         start=True, stop=True),
                                  r=[("swapT",), ("stg", s_xg)], w=[("ps", j2)])
                            P.add("act", lambda e, j1=j1, rsv=rsv: e.activation(
                                out=rsv, in_=ps[j1][:], func=AF.Ln, bias=eps_t, scale=1.0 / 128),
                                r=[("ps", j1), ("small", "eps")], w=[("stg", s_rs)])
                            P.add("act", lambda e, rsv=rsv: e.activation(out=rsv, in_=rsv, func=AF.Exp, scale=-0.5),
                                  r=[("stg", s_rs)], w=[("stg", s_rs)])
                            P.add("dve", lambda e, t1=t1, xg=xg, t=t: e.tensor_tensor(
                                out=t1, in0=xg, in1=cs_sb[:, 0, t * 512:(t + 1) * 512], op=ALU.mult),
                                r=[("stg", s_xg), ("cos",)], w=[("stg", s_t1)])
                            P.add("dve", lambda e, t2=t2, j2=j2, t=t: e.tensor_tensor(
                                out=t2, in0=ps[j2][:], in1=cs_sb[:, 1, t * 512:(t + 1) * 512], op=ALU.mult),
                                r=[("ps", j2), ("sin",)], w=[("stg", s_t2)])
                            P.add("dve", lambda e, t1=t1, t2=t2: e.tensor_tensor(out=t1, in0=t1, in1=t2, op=ALU.add),
                                  r=[("stg", s_t1), ("stg", s_t2)], w=[("stg", s_t1)])
                            P.add("dve", lambda e, t1=t1, rsv=rsv, sqb=sqb: e.tensor_tensor(
                                out=sqb, in0=t1, in1=rsv, op=ALU.mult),
                                r=[("stg", s_t1), ("stg", s_rs)], w=[("stg", s_sq)])
                            dd = dst_d[hd * 128:(hd + 1) * 128, :] if nm == "qaT" else \
                                dst_d[hd // 4][(hd % 4) * 128:(hd % 4 + 1) * 128, :]
                            P.add("sp", lambda e, sqb=sqb, t=t, dd=dd: e.dma_start(
                                out=dd[:, T0 + t * 512:T0 + (t + 1) * 512], in_=sqb),
                                r=[("stg", s_sq)], w=[(nm, hd, hp, t)], dma=True)
            for (c0, dst_d, nm, fn_) in (((C_VG, vg_d, "vg", AF.Copy), (C_RG, rg_d, "rg", AF.Silu),
                                         (C_VA, va_d, "va", AF.Copy)) if "no_wt" not in PHASES else ()):
                for q4 in range(4):
                    b = load_w256(win_d, c0 + q4 * 256)
                    for i in range(8):
                        j = next_ps()
                        for k in range(16):
                            P.add("pe", lambda e, k=k, j=j, i=i, b=b: e.matmul(
                                ps[j][:, 0:256], lhsT=hT[:, k, i * 128:(i + 1) * 128], rhs=W256[b][:, k, :],
                                start=(k == 0), stop=(k == 15)),
                                r=[("W", b), ("hT", i)], w=[("ps", j)])
                        s = next_stg()
                        sv = stg[s][:, 0:256]
                        P.add("act", lambda e, j=j, sv=sv, fn_=fn_: e.activation(out=sv, in_=ps[j][:, 0:256], func=fn_),
                              r=[("ps", j)], w=[("stg", s)])
                        dd = dst_d[hp][i * 128:(i + 1) * 128, :] if nm == "va" else \
                            dst_d[T0 + i * 128:T0 + (i + 1) * 128, :]
                        P.add("sp", lambda e, sv=sv, q4=q4, dd=dd: e.dma_start(
                            out=dd[:, q4 * 256:(q4 + 1) * 256], in_=sv),
                            r=[("stg", s)], w=[(nm, hp, i, q4)], dma=True)

        def all_keys(nm, *dims):
            import itertools
            return [(nm,) + t for t in itertools.product(*[range(d) for d in dims])]

        def phaseB():
            B = Arena(arena_t, AB)
            fac = []
            for hd in range(4):
                fac.append(dict(qq=B.alloc([128, NT], BF16), kk=B.alloc([128, NT], BF16),
                                qi=B.alloc([128, NT], BF16), kst=B.alloc([128, 16, 128], BF16),
                                dec=B.alloc([128, 16], F32)))
            Sf = B.alloc([128, 256], F32)
            Sb = B.alloc([128, 256], BF16)
            Sin = B.alloc([128, 256], F32)
            mark = B.off
            p1 = [dict(qf=B.alloc([128, NT], F32), kf=B.alloc([128, NT], F32), d1=B.alloc([128, NT], F32),
                       vsb=B.alloc([128, 16, 256], BF16)) for _ in range(2)]
            bb, d4 = B.alloc([128, NT], F32), B.alloc([128, NT], F32)
            exs = [B.alloc([128, NT], F32) for _ in range(2)]
            rmask = B.alloc([128, NT], F32)
            P.add("dve", lambda e: e.memset(rmask, 1.0), w=[("rmask",)])
            P.add("dve", lambda e: e.memset(rmask[:, 0:NT:128], 0.0), w=[("rmask",)])
            SC = 128 ** -0.5

            def p1_loads(hd):
                si = hd % 2
                S_ = p1[si]
                P.add("sp", lambda e: e.dma_start(out=S_["qf"], in_=qgT_d[hd * 128:(hd + 1) * 128, :]),
                      r=all_keys("qgT", 4, 2, 2), w=[("qf", si)], dma=True)
                P.add("sp", lambda e: e.dma_start(out=S_["kf"], in_=kgT_d[hd * 128:(hd + 1) * 128, :]),
                      r=all_keys("kgT", 4, 2, 2), w=[("kf", si)], dma=True)
                P.add("sp", lambda e: e.dma_start(out=S_["d1"], in_=laT_d[hd * 128:(hd + 1) * 128, :]),
                      r=all_keys("laT", 4, 2, 2), w=[("d1", si)], dma=True)
                P.add("sp", lambda e: e.dma_start(
                    out=S_["vsb"], in_=vg_d[:, hd * 256:(hd + 1) * 256].rearrange("(i p) c -> p i c", p=128)),
                    r=all_keys("vg", 2, 8, 4), w=[("vsb1", si)], dma=True)

            def p1_compute(hd):
                si = hd % 2
                S_ = p1[si]
                f = fac[hd]
                qf, kf, d1, vsb = S_["qf"], S_["kf"], S_["d1"], S_["vsb"]
                P.add("act", lambda e: e.activation(out=d1, in_=d1, func=AF.Identity, scale=-1.0 / 16.0),
                      r=[("d1", si)], w=[("d1", si)])
                P.add("dve", lambda e: e.tensor_tensor_scan(out=bb, data0=rmask, data1=d1, initial=0.0,
                                                            op0=ALU.mult, op1=ALU.add),
                      r=[("d1", si), ("rmask",)], w=[("bb",)])
                b3 = bb.rearrange("p (i t) -> p i t", i=16)
                d13 = d1.rearrange("p (i t) -> p i t", i=16)
                d43 = d4.rearrange("p (i t) -> p i t", i=16)
                bref = b3[:, :, 63:64].broadcast_to([128, 16, 128])
                blast = b3[:, :, 127:128].broadcast_to([128, 16, 128])
                P.add("dve", lambda e: e.tensor_tensor(out=d13, in0=b3, in1=bref, op=ALU.subtract),
                      r=[("bb",)], w=[("d1", si)])
                P.add("dve", lambda e: e.tensor_tensor(out=d43, in0=blast, in1=b3, op=ALU.subtract),
                      r=[("bb",)], w=[("d4",)])
                P.add("act", lambda e: e.activation(out=exs[0], in_=d1, func=AF.Exp), r=[("d1", si)], w=[("ex", 0)])
                P.add("act", lambda e: e.activation(out=exs[1], in_=d1, func=AF.Exp, scale=-1.0),
                      r=[("d1", si)], w=[("ex", 1)])
                P.add("dve", lambda e: e.scalar_tensor_tensor(out=f["qq"], in0=qf, scalar=SC, in1=exs[0],
                                                              op0=ALU.mult, op1=ALU.mult),
                      r=[("qf", si), ("ex", 0)], w=[("fac", hd, "qq")])
                P.add("dve", lambda e: e.tensor_tensor(out=f["kk"], in0=kf, in1=exs[1], op=ALU.mult),
                      r=[("kf", si), ("ex", 1)], w=[("fac", hd, "kk")])
                P.add("act", lambda e: e.activation(out=exs[0], in_=bb, func=AF.Exp), r=[("bb",)], w=[("ex", 0)])
                P.add("act", lambda e: e.activation(out=f["dec"], in_=b3[:, :, 127], func=AF.Exp),
                      r=[("bb",)], w=[("fac", hd, "dec")])
                P.add("act", lambda e: e.activation(out=exs[1], in_=d4, func=AF.Exp), r=[("d4",)], w=[("ex", 1)])
                P.add("dve", lambda e: e.scalar_tensor_tensor(out=f["qi"], in0=qf, scalar=SC, in1=exs[0],
                                                              op0=ALU.mult, op1=ALU.mult),
                      r=[("qf", si), ("ex", 0)], w=[("fac", hd, "qi")])
                ksb = d4.bitcast(BF16)[:, 0:NT]
                P.add("dve", lambda e: e.tensor_tensor(out=ksb, in0=kf, in1=exs[1], op=ALU.mult),
                      r=[("kf", si), ("ex", 1)], w=[("d4",)])
                for half in range(2):
                    j = next_ps()
                    pb = ps[j][:].bitcast(BF16)
                    for ii in range(8):
                        i = half * 8 + ii
                        P.add("pe", lambda e, i=i, ii=ii, pb=pb: e.transpose(
                            out=pb[:, ii * 128:(ii + 1) * 128], in_=ksb[:, i * 128:(i + 1) * 128],
                            identity=ident[:]),
                            r=[("d4",), ("ident",)], w=[("ps", j)])
                    P.add("act", lambda e, pb=pb, half=half: e.activation(
                        out=f["kst"][:, half * 8:(half + 1) * 8, :],
                        in_=pb.rearrange("p (i d) -> p i d", i=8), func=AF.Copy),
                        r=[("ps", j)], w=[("fac", hd, "kst")])
                P.add("dve", lambda e: e.memset(Sf, 0.0), w=[("Sf",)])
                for i in range(16):
                    j = next_ps()
                    P.add("pe", lambda e, j=j, i=i: e.matmul(ps[j][:, 0:256], lhsT=f["kst"][:, i, :],
                                                             rhs=vsb[:, i, :], start=True, stop=True),
                          r=[("fac", hd, "kst"), ("vsb1", si)], w=[("ps", j)])
                    P.add("dve", lambda e, j=j, i=i: e.scalar_tensor_tensor(
                        out=Sf, in0=Sf, scalar=f["dec"][:, i:i + 1], in1=ps[j][:, 0:256],
                        op0=ALU.mult, op1=ALU.add),
                        r=[("ps", j), ("Sf",), ("fac", hd, "dec")], w=[("Sf",)])
                P.add("sp", lambda e: e.dma_start(out=st_d[hd * 128:(hd + 1) * 128, :], in_=Sf),
                      r=[("Sf",)], w=[("st_d", hd)], dma=True)

            p1_loads(0)
            for hd in range(4):
                if hd + 1 < 4:
                    p1_loads(hd + 1)
                p1_compute(hd)
            if "noBx" in PHASES:
                return
            for hg in range(2):
                P.add("pool", lambda e, hg=hg: e.collective_compute(
                    "AllGather", ALU.bypass, replica_groups=RGROUPS,
                    ins=[kaT_d[hg].opt()], outs=[kaT_g[hg].opt()]),
                    r=all_keys("kaT", 8, 2, 2), w=[("kaT_g", hg)], kind="cc")
            for hv in range(2):
                P.add("pool", lambda e, hv=hv: e.collective_compute(
                    "AllGather", ALU.bypass, replica_groups=RGROUPS,
                    ins=[va_d[hv].opt()], outs=[va_g[hv].opt()]),
                    r=all_keys("va", 2, 8, 4), w=[("va_g", hv)], kind="cc")
            P.add("pool", lambda e: e.collective_compute(
                "AllGather", ALU.bypass, replica_groups=RGROUPS,
                ins=[st_d.opt()], outs=[st_g.opt()]),
                r=[("st_d", h_) for h_ in range(4)], w=[("st_g",)], kind="cc")
            barrier()
            if "noBa" in PHASES:
                return
            B.off = mark
            sets = []
            for s_ in range(2):
                sets.append(dict(
                    qT=B.alloc([128, NT], BF16), kT=B.alloc([128, NT], BF16), kTp=B.alloc([128, NT], BF16),
                    Vr={1: B.alloc([128, 16, 128], BF16), 4: B.alloc([128, 16, 128], BF16),
                        16: B.alloc([128, 16, 128], BF16)},
                    Vp={1: B.alloc([128, 1, 128], BF16), 4: B.alloc([128, 4, 128], BF16),
                        16: B.alloc([128, 16, 128], BF16)}))
            accs = [B.alloc([128, 2, NT], F32) for _ in range(2)]
            PT = [B.alloc([128, 512], BF16) for _ in range(3)]
            oTb = [B.alloc([128, NT], BF16) for _ in range(2)]
            SCA = 128 ** -0.5

            def att_loads(hd):
                si = hd % 2
                S_ = sets[si]
                P.add("sp", lambda e: e.dma_start(out=S_["qT"], in_=qaT_d[hd * 128:(hd + 1) * 128, :]),
                      r=all_keys("qaT", 8, 2, 2), w=[("qT", si)], dma=True)
                P.add("sp", lambda e: e.dma_start(
                    out=S_["kT"], in_=kaT_d[hd // 4][(hd % 4) * 128:(hd % 4 + 1) * 128, :]),
                    r=all_keys("kaT", 8, 2, 2), w=[("kT", si)], dma=True)
                P.add("sp", lambda e: e.dma_start(
                    out=S_["kTp"], in_=kaT_g[hd // 4][(hd % 4) * 128:(hd % 4 + 1) * 128, :]),
                    r=[("kaT_g", hd // 4)], w=[("kTp", si)], dma=True)
                hc = slice(hd * 128, (hd + 1) * 128)
                for r_ in (1, 4, 16):
                    NL = 16 // r_
                    for hv in range(2):
                        for (srcT, dstT, rk, wk, prev) in (
                                (va_d[hv], S_["Vr"][r_], all_keys("va", 2, 8, 4), ("Vr", r_, si), False),
                                (va_g[hv][0:HP, :], S_["Vp"][r_], [("va_g", hv)], ("Vp", r_, si), True)):
                            src = srcT[:, hc]
                            if r_ == 16:
                                src = src.rearrange("(j r) c -> j r c", r=16)
                                dst = dstT[hv * 64:(hv + 1) * 64, :, :]
                            elif r_ == 1:
                                src = src.rearrange("(n j) c -> j n c", j=128)
                                if prev:
                                    if hv == 0:
                                        continue
                                    src = src[:, 7:8, :]
                                    dst = dstT[:, 0:1, :]
                                else:
                                    dst = dstT[:, hv * 8:(hv + 1) * 8, :]
                            else:
                                NLh = NL // 2
                                src4 = src.rearrange("(n j r) c -> n j r c", j=128, r=r_)
                                dst4 = dstT if prev else dstT.rearrange("p (r n) c -> p r n c", r=r_)
                                if prev:
                                    if hv == 0:
                                        continue
                                    P.add("sp", lambda e, src=src4[NLh - 1], dst=dst4[:, :, :]: e.dma_start(
                                        out=dst, in_=src), r=rk, w=[wk], dma=True)
                                else:
                                    for n in range(NLh):
                                        P.add("sp", lambda e, src=src4[n], dst=dst4[:, :, hv * NLh + n, :]:
                                              e.dma_start(out=dst, in_=src), r=rk, w=[wk], dma=True)
                                continue
                            P.add("sp", lambda e, src=src, dst=dst: e.dma_start(out=dst, in_=src),
                                  r=rk, w=[wk], dma=True)

            def att_compute(hd):
                si = hd % 2
                S_ = sets[si]
                qT, kT, kTp, Vr, Vp = S_["qT"], S_["kT"], S_["kTp"], S_["Vr"], S_["Vp"]
                acc = accs[si]
                G = []
                for r_ in (1, 4, 16):
                    NL = 16 // r_
                    blocks = [(res, nl) for res in range(r_) for nl in range(NL)]
                    for g0 in range(0, 16, 2):
                        pair = blocks[g0:g0 + 2]
                        first = [nl == 0 for (_, nl) in pair]
                        mv = 1 if not (first[0] or first[1]) else (2 if (first[0] and first[1]) else 0)
                        sl = []
                        for (res, nl) in pair:
                            q0 = res + r_ * 128 * nl
                            qs = slice(q0, q0 + r_ * 127 + 1, r_)
                            if nl == 0:
                                p0 = res + r_ * 128 * (NL - 1)
                                kprev = kTp[:, p0:p0 + r_ * 127 + 1:r_]
                                vprev = Vp[r_][:, res, :]
                                kpk, vpk = ("kTp", si), ("Vp", r_, si)
                            else:
                                p0 = res + r_ * 128 * (nl - 1)
                                kprev = kT[:, p0:p0 + r_ * 127 + 1:r_]
                                vprev = Vr[r_][:, res * NL + nl - 1, :]
                                kpk, vpk = ("kT", si), ("Vr", r_, si)
                            vcur = Vr[r_][:, res * NL + nl, :]
                            sl.append((qs, kprev, kpk, vprev, vpk, vcur))
                        G.append(dict(r_=r_, mv=mv, sl=sl))
                NG = len(G)

                def bank(g):
                    gg = hd * NG + g
                    return gg % 3, 3 + gg % 3, PT[gg % 3]

                def S(g):
                    jS, jO, pt = bank(g)
                    mv = G[g]["mv"]
                    P.add("pe", lambda e: e.matmul(ps[jS][:], lhsT=ident[:], rhs=m4[:, mv, :],
                                                   start=True, stop=False),
                          r=[("ident",), ("m4",)], w=[("ps", jS)])
                    for bi, (qs, kprev, kpk, vprev, vpk, vcur) in enumerate(G[g]["sl"]):
                        last = (bi == 1)
                        P.add("pe", lambda e, bi=bi, kprev=kprev, qs=qs: e.matmul(
                            ps[jS][:, (2 * bi) * 128:(2 * bi + 1) * 128], lhsT=kprev, rhs=qT[:, qs],
                            start=False, stop=False),
                            r=[kpk, ("qT", si)], w=[("ps", jS)])
                        P.add("pe", lambda e, bi=bi, qs=qs, last=last: e.matmul(
                            ps[jS][:, (2 * bi + 1) * 128:(2 * bi + 2) * 128], lhsT=kT[:, qs], rhs=qT[:, qs],
                            start=False, stop=last),
                            r=[("kT", si), ("qT", si)], w=[("ps", jS)])

                def E(g):
                    jS, jO, pt = bank(g)
                    P.add("act", lambda e: e.activation(out=pt, in_=ps[jS][:], func=AF.Exp, scale=SCA),
                          r=[("ps", jS)], w=[("PT", jS)])

                def V(g):
                    jS, jO, pt = bank(g)
                    r_ = G[g]["r_"]
                    po = ps[jO][:].rearrange("p (b a q) -> p b a q", b=2, a=2)
                    for bi, (qs, kprev, kpk, vprev, vpk, vcur) in enumerate(G[g]["sl"]):
                        P.add("pe", lambda e, bi=bi, vprev=vprev: e.matmul(
                            po[:, bi, 0, :], lhsT=vprev, rhs=pt[:, (2 * bi) * 128:(2 * bi + 1) * 128],
                            start=True, stop=False),
                            r=[vpk, ("PT", jS)], w=[("ps", jO)])
                        P.add("pe", lambda e, bi=bi, vcur=vcur: e.matmul(
                            po[:, bi, 0, :], lhsT=vcur, rhs=pt[:, (2 * bi + 1) * 128:(2 * bi + 2) * 128],
                            start=False, stop=True),
                            r=[("Vr", r_, si), ("PT", jS)], w=[("ps", jO)])
                        P.add("pe", lambda e, bi=bi: e.matmul(
                            po[:, bi, 1, :], lhsT=ones[:], rhs=pt[:, (2 * bi) * 128:(2 * bi + 1) * 128],
                            start=True, stop=False),
                            r=[("ones",), ("PT", jS)], w=[("ps", jO)])
                        P.add("pe", lambda e, bi=bi: e.matmul(
                            po[:, bi, 1, :], lhsT=ones[:], rhs=pt[:, (2 * bi + 1) * 128:(2 * bi + 2) * 128],
                            start=False, stop=True),
                            r=[("ones",), ("PT", jS)], w=[("ps", jO)])

                def Dv(g):
                    jS, jO, pt = bank(g)
                    r_ = G[g]["r_"]
                    po = ps[jO][:].rearrange("p (b a q) -> p b a q", b=2, a=2)
                    for bi, (qs, kprev, kpk, vprev, vpk, vcur) in enumerate(G[g]["sl"]):
                        if r_ == 1:
                            P.add("act", lambda e, bi=bi, qs=qs: e.activation(
                                out=acc[:, :, qs], in_=po[:, bi, :, :], func=AF.Copy),
                                r=[("ps", jO)], w=[("acc", si)])
                        else:
                            P.add("dve", lambda e, bi=bi, qs=qs: e.tensor_tensor(
                                out=acc[:, :, qs], in0=po[:, bi, :, :], in1=acc[:, :, qs], op=ALU.add),
                                r=[("ps", jO), ("acc", si)], w=[("acc", si)])

                S(0)
                S(1)
                for g in range(NG):
                    E(g)
                    V(g)
                    if g + 2 < NG:
                        S(g + 2)
                    Dv(g)
                ob_ = oTb[si]
                P.add("dve", lambda e: e.reciprocal(out=acc[:, 1, :], in_=acc[:, 1, :]),
                      r=[("acc", si)], w=[("acc", si)])
                P.add("dve", lambda e: e.tensor_tensor(out=ob_, in0=acc[:, 0, :], in1=acc[:, 1, :], op=ALU.mult),
                      r=[("acc", si)], w=[("oTb", si)])
                P.add("sp", lambda e: e.dma_start(out=oT_d[1024 + hd * 128:1024 + (hd + 1) * 128, :], in_=ob_),
                      r=[("oTb", si)], w=[("oT_d", 8 + hd)], dma=True)

            att_loads(0)
            for hd in range(8):
                if hd + 1 < 8:
                    att_loads(hd + 1)
                att_compute(hd)
            barrier()
            if "noB2" in PHASES:
                return
            B.off = mark
            g2 = []
            for s_ in range(2):
                g2.append(dict(vsb=B.alloc([128, 16, 256], BF16), rsb=B.alloc([128, 16, 256], BF16),
                               Sball=B.alloc([128, 16, 256], BF16), oTg=B.alloc([128, 2, NT], BF16),
                               sc=B.alloc([128, 48], F32), Sin=B.alloc([128, 256], F32)))
            ot = [B.alloc([128, 256], F32) for _ in range(4)]
            ob = [B.alloc([128, 256], BF16) for _ in range(4)]
            Sm = [B.alloc([128, 128], BF16) for _ in range(2)]
            junk = B.alloc([128, 256], BF16)

            def g2_loads(hd):
                si = hd % 2
                S_ = g2[si]
                P.add("sp", lambda e: e.dma_start(
                    out=S_["vsb"], in_=vg_d[:, hd * 256:(hd + 1) * 256].rearrange("(i p) c -> p i c", p=128)),
                    r=all_keys("vg", 2, 8, 4), w=[("vsb", si)], dma=True)
                P.add("sp", lambda e: e.dma_start(
                    out=S_["rsb"], in_=rg_d[:, hd * 256:(hd + 1) * 256].rearrange("(i p) c -> p i c", p=128)),
                    r=all_keys("rg", 2, 8, 4), w=[("rsb", si)], dma=True)
                P.add("sp", lambda e: e.dma_start(out=S_["Sin"], in_=st_g[hd * 128:(hd + 1) * 128, :]),
                      r=[("st_g",)], w=[("Sin", si)], dma=True)

            def g2_compute(hd):
                si = hd % 2
                S_ = g2[si]
                f = fac[hd]
                vsb_, rsb_, Sball, oTg, sc = S_["vsb"], S_["rsb"], S_["Sball"], S_["oTg"], S_["sc"]
                P.add("dve", lambda e: e.tensor_scalar(out=Sf, in0=S_["Sin"], scalar1=flag, scalar2=None,
                                                       op0=ALU.mult),
                      r=[("Sin", si), ("small", "flag")], w=[("Sf",)])
                P.add("act", lambda e: e.activation(out=Sball[:, 0, :], in_=Sf, func=AF.Copy),
                      r=[("Sf",)], w=[("Sball", si, 0)])
                P.add("dve", lambda e: e.memset(sc[:, 0:16], 0.0), w=[("sc", si, "ss")])
                for i in range(15):
                    jU = 6 + i % 2
                    P.add("pe", lambda e, jU=jU, i=i: e.matmul(ps[jU][:, 0:256], lhsT=f["kst"][:, i, :],
                                                               rhs=vsb_[:, i, :], start=True, stop=True),
                          r=[("fac", hd, "kst"), ("vsb", si)], w=[("ps", jU)])
                    P.add("dve", lambda e, jU=jU, i=i: e.scalar_tensor_tensor(
                        out=Sf, in0=Sf, scalar=f["dec"][:, i:i + 1], in1=ps[jU][:, 0:256],
                        op0=ALU.mult, op1=ALU.add),
                        r=[("ps", jU), ("Sf",), ("fac", hd, "dec")], w=[("Sf",)])
                    P.add("act", lambda e, i=i: e.activation(out=Sball[:, i + 1, :], in_=Sf, func=AF.Copy),
                          r=[("Sf",)], w=[("Sball", si, i + 1)])

                def St(i):
                    tk = slice(i * 128, (i + 1) * 128)
                    jA = i % 2
                    sm = Sm[i % 2]
                    P.add("pe", lambda e: e.matmul(ps[jA][:, 0:128], lhsT=f["kk"][:, tk], rhs=f["qq"][:, tk],
                                                   start=True, stop=True),
                          r=[("fac", hd, "kk"), ("fac", hd, "qq")], w=[("ps", jA)])
                    P.add("dve", lambda e: e.tensor_tensor(out=sm, in0=ps[jA][:, 0:128], in1=mgla[:], op=ALU.mult),
                          r=[("ps", jA), ("mgla",)], w=[("Sm", i % 2)])

                def Oc(i):
                    tk = slice(i * 128, (i + 1) * 128)
                    jO_ = 2 + i % 3
                    sm = Sm[i % 2]
                    q4 = i % 4
                    P.add("pe", lambda e: e.matmul(ps[jO_][:, 0:256], lhsT=sm, rhs=vsb_[:, i, :],
                                                   start=True, stop=False),
                          r=[("Sm", i % 2), ("vsb", si)], w=[("ps", jO_)])
                    P.add("pe", lambda e: e.matmul(ps[jO_][:, 0:256], lhsT=f["qi"][:, tk], rhs=Sball[:, i, :],
                                                   start=False, stop=True),
                          r=[("fac", hd, "qi"), ("Sball", si, i)], w=[("ps", jO_)])
                    P.add("act", lambda e: e.activation(out=junk, in_=ps[jO_][:, 0:256], func=AF.Square,
                                                        accum_out=sc[:, i:i + 1]),
                          r=[("ps", jO_), ("sc", si, "ss")], w=[("junk",), ("sc", si, "ss", i)])
                    P.add("act", lambda e: e.activation(out=sc[:, 16 + i:17 + i], in_=sc[:, i:i + 1], func=AF.Sqrt,
                                                        bias=eps_t, scale=1.0 / 256),
                          r=[("sc", si, "ss", i), ("small", "eps")], w=[("sc", si, "ss2", i)])
                    P.add("dve", lambda e: e.reciprocal(out=sc[:, 32 + i:33 + i], in_=sc[:, 16 + i:17 + i]),
                          r=[("sc", si, "ss2", i)], w=[("sc", si, "rs", i)])
                    P.add("dve", lambda e: e.scalar_tensor_tensor(out=ot[q4], in0=ps[jO_][:, 0:256],
                                                                  scalar=sc[:, 32 + i:33 + i], in1=goutb[:],
                                                                  op0=ALU.mult, op1=ALU.mult),
                          r=[("ps", jO_), ("sc", si, "rs", i), ("goutb",)], w=[("ot", q4)])
                    P.add("dve", lambda e: e.tensor_tensor(out=ob[q4], in0=ot[q4], in1=rsb_[:, i, :], op=ALU.mult),
                          r=[("ot", q4), ("rsb", si)], w=[("ob", q4)])

                def Tr(i):
                    tk = slice(i * 128, (i + 1) * 128)
                    q4 = i % 4
                    pb = ps[5][:].bitcast(BF16)
                    for c in range(2):
                        P.add("pe", lambda e, c=c: e.transpose(out=pb[:, c * 128:(c + 1) * 128],
                                                               in_=ob[q4][:, c * 128:(c + 1) * 128],
                                                               identity=ident[:]),
                              r=[("ob", q4), ("ident",)], w=[("ps", 5)])
                    P.add("act", lambda e: e.activation(
                        out=oTg[:, :, tk], in_=pb[:, 0:256].rearrange("p (c t) -> p c t", c=2), func=AF.Copy),
                        r=[("ps", 5)], w=[("oTg", si)])

                for s_ in range(16 + 3):
                    if s_ < 16:
                        St(s_)
                    if 0 <= s_ - 1 < 16:
                        Oc(s_ - 1)
                    if 0 <= s_ - 3 < 16:
                        Tr(s_ - 3)
                for c in range(2):
                    P.add("sp", lambda e, c=c: e.dma_start(
                        out=oT_d[hd * 256 + c * 128:hd * 256 + (c + 1) * 128, :], in_=oTg[:, c, :]),
                        r=[("oTg", si)], w=[("oT_d", hd * 2 + c)], dma=True)

            g2_loads(0)
            for hd in range(4):
                if hd + 1 < 4:
                    g2_loads(hd + 1)
                g2_compute(hd)
            barrier()

        def phaseC(hp):
            T0 = hp * HP
            for k4 in range(4):
                P.add("sp", lambda e, k4=k4: e.dma_start(
                    out=hT[:, k4 * 4:(k4 + 1) * 4, :],
                    in_=oT_d[k4 * 512:(k4 + 1) * 512, T0:T0 + HP].rearrange("(k p) t -> p k t", p=128)),
                    r=[("oT_d", j) for j in range(16)], w=[("hT", i) for i in range(8)], dma=True)
            for i in range(8):
                P.add("sp", lambda e, i=i: e.dma_start(out=xs[i], in_=x1_d[T0 + i * 128:T0 + (i + 1) * 128, :]),
                      r=[("x1_d", hp, i)], w=xk(i), dma=True)
            for dgp in range(8):
                b = load_w256(wout_d, dgp * 256)
                for i in range(8):
                    j = next_ps()
                    for k in range(16):
                        P.add("pe", lambda e, k=k, j=j, i=i, b=b: e.matmul(
                            ps[j][:, 0:256], lhsT=hT[:, k, i * 128:(i + 1) * 128], rhs=W256[b][:, k, :],
                            start=(k == 0), stop=(k == 15)),
                            r=[("W", b), ("hT", i)], w=[("ps", j)])
                    dgk = dgp // 2
                    P.add("dve", lambda e, j=j, i=i, dgp=dgp: e.tensor_tensor(
                        out=xs[i][:, dgp * 256:(dgp + 1) * 256], in0=ps[j][:, 0:256],
                        in1=xs[i][:, dgp * 256:(dgp + 1) * 256], op=ALU.add),
                        r=[("ps", j), ("xs", i, dgk)], w=[("xs", i, dgk)])
            load_gain(n2_d)
            norm_all()
            ffn(w2g, w2u, w2d)
            for i in range(8):
                P.add("sp", lambda e, i=i: e.dma_start(out=out_d[T0 + i * 128:T0 + (i + 1) * 128, :], in_=xs[i]),
                      r=xk(i), w=[("out", hp, i)], dma=True)

        if "A0" in PHASES:
            phaseA(0)
        if "A1" in PHASES:
            phaseA(1)
        if "B" in PHASES:
            phaseB()
        if "C0" in PHASES:
            phaseC(0)
        if "C1" in PHASES:
            phaseC(1)

        P.schedule()
        P.emit(nc)
    return nc


_NC_CACHE = {}


def _host_consts(h):
    half = 64
    inv = 1.0 / (10000.0 ** (np.arange(half, dtype=np.float32) / half))
    pos = (h * NT + np.arange(NT)).astype(np.float32)
    ang = pos[None, :] * inv[:, None]
    cosT = np.concatenate([np.cos(ang), np.cos(ang)], 0).astype(np.float32)
    sinT = np.concatenate([-np.sin(ang), np.sin(ang)], 0).astype(np.float32)
    j = np.arange(128)[:, None]
    i = np.arange(128)[None, :]
    mcur = np.where(j <= i, 0.0, NEG).astype(np.float32)
    mprev = np.where(j >= i, 0.0, NEG).astype(np.float32)
    mprevh = mprev if h == 1 else np.full((128, 128), NEG, np.float32)
    m4 = np.stack([np.concatenate([mprevh, mcur, mprev, mcur], 1),
                   np.concatenate([mprev, mcur, mprev, mcur], 1),
                   np.concatenate([mprevh, mcur, mprevh, mcur], 1)], 0).astype(np.float32)
    mgla = (j <= i).astype(np.float32)
    ident = np.eye(128, dtype=np.float32)
    swapT = np.zeros((128, 128), np.float32)
    swapT[(np.arange(128) + 64) % 128, np.arange(128)] = 1.0
    flag = np.full((128, 1), float(h), np.float32)
    return dict(cosT=cosT, sinT=sinT, m4=m4, mgla=mgla, ident=ident, swapT=swapT, flag=flag)


def kernel(x, ffn1_norm, ffn1_w_gate, ffn1_w_up, ffn1_w_down, mix_norm, w_in,
           gla_gate_up, gla_gate_bias, gla_out_norm, att_q_norm, att_k_norm, w_out,
           ffn2_norm, ffn2_w_gate, ffn2_w_up, ffn2_w_down):
    f = lambda a: np.ascontiguousarray(np.asarray(a, dtype=np.float32))
    x = f(x)
    common = dict(
        w1g=f(ffn1_w_gate[0]), w1u=f(ffn1_w_up[0]), w1d=f(ffn1_w_down[0]),
        w2g=f(ffn2_w_gate[0]), w2u=f(ffn2_w_up[0]), w2d=f(ffn2_w_down[0]),
        win=f(w_in[0]), wout=f(w_out[0]),
        n1=f(ffn1_norm[0]).reshape(1, D), nm=f(mix_norm[0]).reshape(1, D), n2=f(ffn2_norm[0]).reshape(1, D),
        gout=f(gla_out_norm[0]).reshape(1, 256),
        gq=f(att_q_norm[0]).reshape(128, 1), gk=f(att_k_norm[0]).reshape(128, 1),
        gup=f(gla_gate_up[0]),
        gbias=f(np.asarray(gla_gate_bias[0]).reshape(4, 128).T),
    )
    consts = [_host_consts(0), _host_consts(1)]
    in_maps = []
    for c in range(8):
        b, h = c // 2, c % 2
        m = dict(common)
        m["x"] = np.ascontiguousarray(x[b, h * NT:(h + 1) * NT, :])
        m.update(consts[h])
        in_maps.append(m)
    if "nc" not in _NC_CACHE:
        _NC_CACHE["nc"] = build_nc()
    nc = _NC_CACHE["nc"]
    res = run_bass_kernel_spmd(nc, in_maps, core_ids=list(range(8)))
    out = np.empty((4, 4096, D), np.float32)
    for c in range(8):
        b, h = c // 2, c % 2
        out[b, h * NT:(h + 1) * NT, :] = res.results[c]["out"]
    return out
```

```python
import numpy as np
import concourse.bass as bass
import concourse.mybir as mybir
from concourse.bass_utils import run_bass_kernel_spmd

F32 = mybir.dt.float32
BF16 = mybir.dt.bfloat16
AF = mybir.ActivationFunctionType
ALU = mybir.AluOpType

D = 2048
DFF = 5632
DIN = 6160
NT = 2048
HP = 1024
EPS = 1e-6
NEG = -30000.0
NR = 8
PHASES = "A0 A1 B C0 C1"
DEBUG_OUT = set()
RGROUPS = [[0, 1], [2, 3], [4, 5], [6, 7]]
C_QG, C_KG, C_VG, C_GL, C_RG, C_QA, C_KA, C_VA = 0, 512, 1024, 2048, 2064, 3088, 4112, 5136


class Prog:
    def __init__(self):
        self.ops = []
        self.bar = None

    def add(self, eng, fn, r=(), w=(), dma=False, kind=None, nobar=False):
        self.ops.append(dict(eng=eng, fn=fn, r=list(r), w=list(w), dma=dma, kind=kind,
                             bar=(None if nobar else self.bar), signal=False))
        return len(self.ops) - 1

    def barrier(self, fn):
        i = self.add("dve", fn, kind="barrier")
        self.bar = i

    def schedule(self):
        ops = self.ops
        last_w, readers = {}, {}
        last_eng = {}
        dma_hist = {"sp": [], "pool": []}
        for i, op in enumerate(ops):
            deps = set()
            if op["kind"] == "barrier":
                for e, j in last_eng.items():
                    deps.add(j)
                for e in dma_hist:
                    deps.update(dma_hist[e][-NR:])
            else:
                for k in op["r"]:
                    if k in last_w:
                        deps.add(last_w[k])
                for k in op["w"]:
                    if k in last_w:
                        deps.add(last_w[k])
                    deps.update(readers.get(k, ()))
                if op["bar"] is not None:
                    deps.add(op["bar"])
            deps.discard(i)
            for k in op["r"]:
                readers.setdefault(k, []).append(i)
            for k in op["w"]:
                last_w[k] = i
                readers[k] = []
            dl = []
            for j in deps:
                pj = ops[j]
                if (not pj["dma"]) and (not op["dma"]) and pj["eng"] == "pe" and op["eng"] == "pe" \
                        and pj["kind"] != "cc":
                    continue
                dl.append(j)
            op["deps"] = dl
            if op["dma"]:
                dma_hist[op["eng"]].append(i)
            else:
                last_eng[op["eng"]] = i
        for op in ops:
            for j in op["deps"]:
                if not ops[j]["dma"]:
                    ops[j]["signal"] = True
        cnt = {}
        dcnt = {"sp": 0, "pool": 0}
        ncc = 0
        for op in ops:
            if op["dma"]:
                idx = dcnt[op["eng"]]
                dcnt[op["eng"]] += 1
                op["sem"] = ("ring", op["eng"], idx % NR)
                op["val"] = 16 * (idx // NR + 1)
            elif op["kind"] == "cc":
                op["sem"] = ("cc", ncc)
                op["val"] = 1
                ncc += 1
            elif op["signal"]:
                cnt[op["eng"]] = cnt.get(op["eng"], 0) + 1
                op["sem"] = ("eng", op["eng"])
                op["val"] = cnt[op["eng"]]
        self.ncc = ncc
        for op in ops:
            waits = {}
            for j in op["deps"]:
                pj = ops[j]
                s, v = pj["sem"], pj["val"]
                if waits.get(s, 0) < v:
                    waits[s] = v
            if op["dma"] and op["val"] > 16:
                s = op["sem"]
                waits[s] = max(waits.get(s, 0), op["val"] - 16)
            op["waits"] = waits
        self.final = {}
        for op in ops:
            if op["dma"]:
                self.final[op["sem"]] = op["val"]

    def emit(self, nc):
        ops = self.ops
        sems = {}
        import contextlib
        with contextlib.ExitStack() as st:
            for e in ("pe", "act", "dve"):
                sems[("eng", e)] = st.enter_context(nc.semaphore("s_" + e))
            for e in ("sp", "pool"):
                for k in range(NR):
                    sems[("ring", e, k)] = st.enter_context(nc.semaphore("r_%s%d" % (e, k)))
            for k in range(self.ncc):
                sems[("cc", k)] = st.enter_context(nc.semaphore("cc%d" % k))
            block = st.enter_context(nc.Block())

            def run(name, e):
                waited = {}
                for op in ops:
                    if op["eng"] != name:
                        continue
                    for s, v in op["waits"].items():
                        if waited.get(s, 0) < v:
                            e.wait_ge(sems[s], v)
                            waited[s] = v
                    ins = op["fn"](e)
                    if op["dma"]:
                        ins.then_inc(sems[op["sem"]], 16)
                    elif op["kind"] == "cc":
                        ins.then_inc(sems[op["sem"]], 1)
                    elif op["signal"]:
                        ins.then_inc(sems[op["sem"]], 1)
                if name in ("sp", "pool"):
                    for s, v in self.final.items():
                        if s[1] == name and waited.get(s, 0) < v:
                            e.wait_ge(sems[s], v)

            @block.tensor
            def _(e):
                run("pe", e)

            @block.scalar
            def _(e):
                run("act", e)

            @block.vector
            def _(e):
                run("dve", e)

            @block.gpsimd
            def _(e):
                run("pool", e)

            @block.sync
            def _(e):
                run("sp", e)


class Arena:
    def __init__(self, base_ap, nbytes):
        self.base = base_ap
        self.n = nbytes
        self.off = 0

    def alloc(self, shape, dt):
        assert shape[0] == 128
        n = 1
        for s in shape[1:]:
            n *= s
        nb = n * (4 if dt == F32 else 2)
        nb_al = (nb + 63) // 64 * 64
        assert self.off + nb_al <= self.n, ("arena overflow", self.off, nb_al, self.n)
        ap = self.base[:, self.off // 2:(self.off + nb) // 2]
        self.off += nb_al
        if dt == F32:
            ap = ap.bitcast(F32)
        if len(shape) == 3:
            ap = ap.rearrange("p (a b) -> p a b", a=shape[1])
        elif len(shape) == 4:
            ap = ap.rearrange("p (a b c) -> p a b c", a=shape[1], b=shape[2])
        return ap


def build_nc():
    nc = bass.Bass("TRN2", target_bir_lowering=False)
    P = Prog()

    def din(name, shape, dt=F32):
        return nc.dram_tensor(name, list(shape), dt, kind="ExternalInput").ap()

    def dscr(name, shape, dt):
        if name in DEBUG_OUT:
            return nc.dram_tensor(name, list(shape), dt, kind="ExternalOutput").ap()
        return nc.dram_tensor(name, list(shape), dt).ap()

    x_d = din("x", [NT, D])
    w1g, w1u, w1d = din("w1g", [D, DFF]), din("w1u", [D, DFF]), din("w1d", [DFF, D])
    w2g, w2u, w2d = din("w2g", [D, DFF]), din("w2u", [D, DFF]), din("w2d", [DFF, D])
    win_d, wout_d = din("win", [D, DIN]), din("wout", [D, D])
    n1_d, nm_d, n2_d = din("n1", [1, D]), din("nm", [1, D]), din("n2", [1, D])
    gout_d = din("gout", [1, 256])
    gq_d, gk_d = din("gq", [128, 1]), din("gk", [128, 1])
    gup_d = din("gup", [16, 512])
    gbias_d = din("gbias", [128, 4])
    cos_d, sin_d = din("cosT", [128, NT]), din("sinT", [128, NT])
    m4_d = din("m4", [3, 128, 512])
    mgla_d = din("mgla", [128, 128])
    ident_d = din("ident", [128, 128])
    swap_d = din("swapT", [128, 128])
    flag_d = din("flag", [128, 1])
    out_d = nc.dram_tensor("out", [NT, D], F32, kind="ExternalOutput").ap()

    x1_d = dscr("x1s", [NT, D], F32)
    qgT_d = dscr("qgT", [512, NT], F32)
    kgT_d = dscr("kgT", [512, NT], F32)
    laT_d = dscr("laT", [512, NT], F32)
    vg_d = dscr("vgs", [NT, 1024], BF16)
    rg_d = dscr("rgs", [NT, 1024], BF16)
    qaT_d = dscr("qaT", [1024, NT], BF16)
    kaT_d = [dscr("kaT%d" % i, [512, NT], BF16) for i in range(2)]
    kaT_g = [dscr("kaTg%d" % i, [1024, NT], BF16) for i in range(2)]
    va_d = [dscr("vas%d" % i, [HP, 1024], BF16) for i in range(2)]
    va_g = [dscr("vag%d" % i, [2 * HP, 1024], BF16) for i in range(2)]
    st_d = dscr("sts", [512, 256], F32)
    st_g = dscr("stg", [1024, 256], F32)
    oT_d = dscr("oTs", [D, NT], BF16)

    import contextlib
    with contextlib.ExitStack() as st:
        def sb(name, shape, dt):
            return st.enter_context(nc.sbuf_tensor("sb_" + name, list(shape), dt))

        AB = 172 * 1024
        arena_t = sb("arena", [128, AB // 2], BF16)
        gbc = sb("gbc", [128, D], F32)
        ident = sb("ident", [128, 128], BF16)
        swapT = sb("swapT", [128, 128], BF16)
        ones = sb("ones", [128, 128], BF16)
        mgla = sb("mgla", [128, 128], BF16)
        m4 = sb("m4", [128, 3, 512], BF16)
        gup = sb("gup", [16, 512], F32)
        small = sb("small", [128, 64], F32)
        wgl = sb("wgl", [128, 16, 16], BF16)
        goutb = sb("goutb", [128, 256], F32)
        ps = [st.enter_context(nc.psum_tensor("ps%d" % j, [128, 512], F32)) for j in range(8)]

        negb = small[:, 0:4]
        gq, gk = small[:, 4:5], small[:, 5:6]
        eps_t, one_t, flag = small[:, 6:7], small[:, 7:8], small[:, 8:9]
        ss = [small[:, 10 + j:11 + j] for j in range(2)]
        ss2 = [small[:, 12 + j:13 + j] for j in range(2)]
        rs = [small[:, 14 + j:15 + j] for j in range(2)]
        bar_t = small[:, 16:17]
        gbias_t = small[:, 20:24]

        def barrier():
            P.barrier(lambda e: e.memset(bar_t, 0.0))

        A = Arena(arena_t, AB)
        xs = [A.alloc([128, D], F32) for _ in range(8)]
        hT = A.alloc([128, 16, HP], BF16)
        W256 = [A.alloc([128, 16, 256], BF16) for _ in range(4)]
        tmp_off = A.off
        Wd = [A.alloc([128, 2, D], BF16) for _ in range(2)]
        actT = [A.alloc([128, 2, HP], BF16) for _ in range(2)]
        sg = [A.alloc([128, 512], F32) for _ in range(2)]
        hbs = [A.alloc([128, D], BF16) for _ in range(2)]
        jk = A.alloc([128, D], BF16)
        assert A.off <= AB
        A5 = Arena(arena_t, AB)
        A5.off = tmp_off
        stg = [A5.alloc([128, 1024], BF16) for _ in range(16)]
        cs_sb = A5.alloc([128, 2, HP], F32)

        def xk(i):
            return [("xs", i, d) for d in range(4)]

        def hk(t):
            return [("hT", i) for i in range(4 * t, 4 * t + 4)]

        wcnt = [0]
        wnobar = [False]
        pscnt = [0]

        def next_ps():
            j = pscnt[0] % 8
            pscnt[0] += 1
            return j

        P.add("sp", lambda e: e.dma_start(out=gup[:], in_=gup_d), w=[("gup",)], dma=True)
        P.add("sp", lambda e: e.dma_start(out=gbias_t, in_=gbias_d), w=[("small", "gb")], dma=True)
        P.add("sp", lambda e: e.dma_start(out=gq, in_=gq_d), w=[("small", "gq")], dma=True)
        P.add("sp", lambda e: e.dma_start(out=gk, in_=gk_d), w=[("small", "gk")], dma=True)
        P.add("sp", lambda e: e.dma_start(out=flag, in_=flag_d), w=[("small", "flag")], dma=True)
        P.add("sp", lambda e: e.dma_start(out=goutb[:], in_=gout_d.broadcast_to([128, 256])),
              w=[("goutb",)], dma=True)
        P.add("pool", lambda e: e.dma_start(out=ident[:], in_=ident_d), w=[("ident",)], dma=True)
        P.add("pool", lambda e: e.dma_start(out=swapT[:], in_=swap_d), w=[("swapT",)], dma=True)
        P.add("pool", lambda e: e.dma_start(out=mgla[:], in_=mgla_d), w=[("mgla",)], dma=True)
        P.add("pool", lambda e: e.dma_start(out=m4[:], in_=m4_d.rearrange("a p f -> p a f")),
              w=[("m4",)], dma=True)
        P.add("pool", lambda e: e.dma_start(
            out=wgl[:], in_=win_d.rearrange("(k p) f -> p k f", p=128)[:, :, C_GL:C_GL + 16]),
            w=[("wgl",)], dma=True)
        P.add("dve", lambda e: e.memset(ones[:], 1.0), w=[("ones",)])
        P.add("dve", lambda e: e.memset(eps_t, EPS), w=[("small", "eps")])
        P.add("dve", lambda e: e.memset(one_t, 1.0), w=[("small", "one")])
        P.add("dve", lambda e: e.tensor_scalar(out=negb, in0=gbias_t, scalar1=-1.0, scalar2=None,
                                               op0=ALU.mult),
              r=[("small", "gb")], w=[("small", "negb")])

        def load_gain(g_d):
            P.add("sp", lambda e: e.dma_start(out=gbc[:], in_=g_d.broadcast_to([128, D])),
                  w=[("gbc",)], dma=True)

        def norm_stats(i):
            q = i % 2
            hb = hbs[q]
            P.add("dve", lambda e: e.memset(ss[q], 0.0), w=[("ss", q)])
            P.add("act", lambda e: e.activation(out=jk, in_=xs[i], func=AF.Square, accum_out=ss[q]),
                  r=xk(i), w=[("jk",), ("ss", q)])
            P.add("act", lambda e: e.activation(out=ss2[q], in_=ss[q], func=AF.Sqrt, bias=eps_t,
                                                scale=1.0 / D),
                  r=[("ss", q), ("small", "eps")], w=[("ss2", q)])
            P.add("dve", lambda e: e.reciprocal(out=rs[q], in_=ss2[q]), r=[("ss2", q)], w=[("rs", q)])
            P.add("dve", lambda e: e.scalar_tensor_tensor(out=hb, in0=xs[i], scalar=rs[q], in1=gbc[:],
                                                          op0=ALU.mult, op1=ALU.mult),
                  r=xk(i) + [("rs", q), ("gbc",)], w=[("hb", q)])

        def norm_tr(i):
            q = i % 2
            hb = hbs[q]
            for half in range(2):
                j = next_ps()
                pb = ps[j][:].bitcast(BF16)
                for kk in range(8):
                    k = half * 8 + kk
                    P.add("pe", lambda e, k=k, kk=kk, pb=pb: e.transpose(
                        out=pb[:, kk * 128:(kk + 1) * 128], in_=hb[:, k * 128:(k + 1) * 128],
                        identity=ident[:]),
                        r=[("hb", q), ("ident",)], w=[("ps", j)])
                src = pb.rearrange("p (k t) -> p k t", k=8)
                dst = hT[:, half * 8:(half + 1) * 8, i * 128:(i + 1) * 128]
                if half == 0:
                    P.add("act", lambda e, src=src, dst=dst: e.activation(out=dst, in_=src, func=AF.Copy),
                          r=[("ps", j)], w=[("hT", i)])
                else:
                    P.add("dve", lambda e, src=src, dst=dst: e.tensor_copy(out=dst, in_=src),
                          r=[("ps", j)], w=[("hT", i)])

        def norm_all(pre=None):
            for s_ in range(9):
                if s_ < 8:
                    if pre is not None:
                        pre(s_)
                    norm_stats(s_)
                if s_ >= 1:
                    norm_tr(s_ - 1)

        def load_w256(w_d, c0):
            b = wcnt[0] % 4
            wcnt[0] += 1
            P.add("pool", lambda e: e.dma_start(
                out=W256[b], in_=w_d.rearrange("(k p) f -> p k f", p=128)[:, :, c0:c0 + 256]),
                w=[("W", b)], dma=True, nobar=wnobar[0])
            return b

        def ffn(wg_d, wu_d, wd_d):
            NG = DFF // 256
            st_ = {}

            def gu(g):
                bg = load_w256(wg_d, g * 256)
                bu = load_w256(wu_d, g * 256)
                bd = g % 2
                P.add("pool", lambda e: e.dma_start(
                    out=Wd[bd], in_=wd_d[g * 256:(g + 1) * 256, :].rearrange("(c p) d -> p c d", p=128)),
                    w=[("Wd", bd)], dma=True)
                for c in range(2):
                    for t in range(2):
                        q = (2 * c + t) % 2
                        jg, ju = q, 2 + q
                        for k in range(16):
                            P.add("pe", lambda e, k=k, c=c, t=t, jg=jg: e.matmul(
                                ps[jg][:], lhsT=W256[bg][:, k, c * 128:(c + 1) * 128],
                                rhs=hT[:, k, t * 512:(t + 1) * 512], start=(k == 0), stop=(k == 15)),
                                r=[("W", bg)] + hk(t), w=[("ps", jg)])
                        for k in range(16):
                            P.add("pe", lambda e, k=k, c=c, t=t, ju=ju: e.matmul(
                                ps[ju][:], lhsT=W256[bu][:, k, c * 128:(c + 1) * 128],
                                rhs=hT[:, k, t * 512:(t + 1) * 512], start=(k == 0), stop=(k == 15)),
                                r=[("W", bu)] + hk(t), w=[("ps", ju)])
                        P.add("act", lambda e, jg=jg, q=q: e.activation(out=sg[q], in_=ps[jg][:], func=AF.Silu),
                              r=[("ps", jg)], w=[("sg", q)])
                        P.add("dve", lambda e, ju=ju, q=q, c=c, t=t: e.tensor_tensor(
                            out=actT[bd][:, c, t * 512:(t + 1) * 512], in0=ps[ju][:], in1=sg[q], op=ALU.mult),
                            r=[("ps", ju), ("sg", q)], w=[("actT", bd, c, t)])

            def down(g):
                bd = g % 2
                n = 0
                for i in range(8):
                    for dg in range(4):
                        j = 4 + n % 4
                        n += 1
                        for c in range(2):
                            P.add("pe", lambda e, c=c, i=i, dg=dg, j=j: e.matmul(
                                ps[j][:], lhsT=actT[bd][:, c, i * 128:(i + 1) * 128],
                                rhs=Wd[bd][:, c, dg * 512:(dg + 1) * 512], start=(c == 0), stop=(c == 1)),
                                r=[("actT", bd, c, i // 4), ("Wd", bd)], w=[("ps", j)])
                        P.add("dve", lambda e, i=i, dg=dg, j=j: e.scalar_tensor_tensor(
                            out=xs[i][:, dg * 512:(dg + 1) * 512], in0=ps[j][:], scalar=0.5,
                            in1=xs[i][:, dg * 512:(dg + 1) * 512], op0=ALU.mult, op1=ALU.add),
                            r=[("ps", j), ("xs", i, dg)], w=[("xs", i, dg)])

            gu(0)
            for g in range(1, NG):
                gu(g)
                down(g - 1)
            down(NG - 1)

        def phaseA(hp):
            T0 = hp * HP

            def ld_x(hp_):
                for i in range(8):
                    P.add("sp", lambda e, i=i: e.dma_start(
                        out=xs[i], in_=x_d[hp_ * HP + i * 128:hp_ * HP + (i + 1) * 128, :]),
                        w=xk(i), dma=True)
            if hp == 0:
                ld_x(0)
            load_gain(n1_d)
            norm_all()
            if "noffn" not in PHASES:
                ffn(w1g, w1u, w1d)
            load_gain(nm_d)
            def st_x1(i):
                P.add("sp", lambda e, i=i: e.dma_start(out=x1_d[T0 + i * 128:T0 + (i + 1) * 128, :], in_=xs[i]),
                      r=xk(i), w=[("x1_d", hp, i)], dma=True)
            norm_all(st_x1)
            barrier()
            if hp == 0 and "A1" in PHASES:
                ld_x(1)
            if "nowin" not in PHASES:
                win_phase(hp)
            barrier()

        scnt = [0]

        def next_stg():
            j = scnt[0] % 16
            scnt[0] += 1
            return j

        def win_phase(hp):
            T0 = hp * HP
            P.add("sp", lambda e: e.dma_start(out=cs_sb[:, 0, :], in_=cos_d[:, T0:T0 + HP]), w=[("cos",)], dma=True)
            P.add("sp", lambda e: e.dma_start(out=cs_sb[:, 1, :], in_=sin_d[:, T0:T0 + HP]), w=[("sin",)], dma=True)

            def fm_chunk(b, sub, t, extra_w=()):
                j = next_ps()
                for k in range(16):
                    P.add("pe", lambda e, k=k: e.matmul(
                        ps[j][:], lhsT=W256[b][:, k, sub * 128:(sub + 1) * 128],
                        rhs=hT[:, k, t * 512:(t + 1) * 512], start=(k == 0), stop=(k == 15)),
                        r=[("W", b)] + hk(t), w=[("ps", j)])
                return j

            for (c0, dst_d, nm) in (((C_QG, qgT_d, "qgT"), (C_KG, kgT_d, "kgT")) if "no_wq" not in PHASES else ()):
                for pair in range(2):
                    b = load_w256(win_d, c0 + pair * 256)
                    for sub in range(2):
                        hd = pair * 2 + sub
                        for t in range(2):
                            j = fm_chunk(b, sub, t)
                            s = next_stg()
                            sv = stg[s].bitcast(F32)
                            P.add("act", lambda e, j=j, sv=sv: e.activation(out=sv, in_=ps[j][:], func=AF.Copy),
                                  r=[("ps", j)], w=[("stg", s)])
                            P.add("sp", lambda e, sv=sv, hd=hd, t=t, dst_d=dst_d: e.dma_start(
                                out=dst_d[hd * 128:(hd + 1) * 128, T0 + t * 512:T0 + (t + 1) * 512], in_=sv),
                                r=[("stg", s)], w=[(nm, hd, hp, t)], dma=True)
            for t in (range(2) if "no_wg" not in PHASES else ()):
                j = next_ps()
                for k in range(16):
                    P.add("pe", lambda e, k=k, j=j, t=t: e.matmul(
                        ps[j][0:16, :], lhsT=wgl[:, k, :], rhs=hT[:, k, t * 512:(t + 1) * 512],
                        start=(k == 0), stop=(k == 15)),
                        r=[("wgl",)] + hk(t), w=[("ps", j)])
                s = next_stg()
                gl = stg[s].bitcast(F32)[0:16, :]
                P.add("act", lambda e, j=j, gl=gl: e.activation(out=gl, in_=ps[j][0:16, :], func=AF.Copy),
                      r=[("ps", j)], w=[("stg", s)])
                for hd in range(4):
                    j2 = next_ps()
                    P.add("pe", lambda e, j2=j2, hd=hd, gl=gl: e.matmul(
                        ps[j2][:], lhsT=gup[:, hd * 128:(hd + 1) * 128], rhs=gl, start=True, stop=True),
                        r=[("gup",), ("stg", s)], w=[("ps", j2)])
                    s2 = next_stg()
                    ev = stg[s2].bitcast(F32)
                    P.add("act", lambda e, j2=j2, ev=ev, hd=hd: e.activation(
                        out=ev, in_=ps[j2][:], func=AF.Exp, bias=negb[:, hd:hd + 1], scale=-1.0),
                        r=[("ps", j2), ("small", "negb")], w=[("stg", s2)])
                    P.add("act", lambda e, ev=ev: e.activation(out=ev, in_=ev, func=AF.Ln, bias=one_t, scale=1.0),
                          r=[("stg", s2), ("small", "one")], w=[("stg", s2)])
                    P.add("sp", lambda e, ev=ev, hd=hd, t=t: e.dma_start(
                        out=laT_d[hd * 128:(hd + 1) * 128, T0 + t * 512:T0 + (t + 1) * 512], in_=ev),
                        r=[("stg", s2)], w=[("laT", hd, hp, t)], dma=True)
            for (c0, dst_d, nm, gvec, gkey) in (((C_QA, qaT_d, "qaT", gq, "gq"), (C_KA, kaT_d, "kaT", gk, "gk")) if "no_wa" not in PHASES else ()):
                for pair in range(4):
                    b = load_w256(win_d, c0 + pair * 256)
                    for sub in range(2):
                        hd = pair * 2 + sub
                        for t in range(2):
                            j = fm_chunk(b, sub, t)
                            s_sq, s_xg, s_rs, s_t1, s_t2 = [next_stg() for _ in range(5)]
                            sqb, xg = stg[s_sq][:, 0:512], stg[s_xg][:, 0:512]
                            rsv, t1, t2 = stg[s_rs].bitcast(F32), stg[s_t1].bitcast(F32), stg[s_t2].bitcast(F32)
                            P.add("act", lambda e, j=j, sqb=sqb: e.activation(out=sqb, in_=ps[j][:], func=AF.Square),
                                  r=[("ps", j)], w=[("stg", s_sq)])
                            P.add("act", lambda e, j=j, xg=xg, gvec=gvec: e.activation(
                                out=xg, in_=ps[j][:], func=AF.Identity, scale=gvec),
                                r=[("ps", j), ("small", gkey)], w=[("stg", s_xg)])
                            j1, j2 = next_ps(), next_ps()
                            P.add("pe", lambda e, j1=j1, sqb=sqb: e.matmul(ps[j1][:], lhsT=ones[:], rhs=sqb,
                                                                           start=True, stop=True),
                                  r=[("ones",), ("stg", s_sq)], w=[("ps", j1)])
                            P.add("pe", lambda e, j2=j2, xg=xg: e.matmul(ps[j2][:], lhsT=swapT[:], rhs=xg,
                                                                         start=True, stop=True),
                                  r=[("swapT",), ("stg", s_xg)], w=[("ps", j2)])
                            P.add("act", lambda e, j1=j1, rsv=rsv: e.activation(
                                out=rsv, in_=ps[j1][:], func=AF.Ln, bias=eps_t, scale=1.0 / 128),
                                r=[("ps", j1), ("small", "eps")], w=[("stg", s_rs)])
                            P.add("act", lambda e, rsv=rsv: e.activation(out=rsv, in_=rsv, func=AF.Exp, scale=-0.5),
                                  r=[("stg", s_rs)], w=[("stg", s_rs)])
                            P.add("dve", lambda e, t1=t1, xg=xg, t=t: e.tensor_tensor(
                                out=t1, in0=xg, in1=cs_sb[:, 0, t * 512:(t + 1) * 512], op=ALU.mult),
                                r=[("stg", s_xg), ("cos",)], w=[("stg", s_t1)])
                            P.add("dve", lambda e, t2=t2, j2=j2, t=t: e.tensor_tensor(
                                out=t2, in0=ps[j2][:], in1=cs_sb[:, 1, t * 512:(t + 1) * 512], op=ALU.mult),
                                r=[("ps", j2), ("sin",)], w=[("stg", s_t2)])
                            P.add("dve", lambda e, t1=t1, t2=t2: e.tensor_tensor(out=t1, in0=t1, in1=t2, op=ALU.add),
                                  r=[("stg", s_t1), ("stg", s_t2)], w=[("stg", s_t1)])
                            P.add("dve", lambda e, t1=t1, rsv=rsv, sqb=sqb: e.tensor_tensor(
                                out=sqb, in0=t1, in1=rsv, op=ALU.mult),
                                r=[("stg", s_t1), ("stg", s_rs)], w=[("stg", s_sq)])
                            dd = dst_d[hd * 128:(hd + 1) * 128, :] if nm == "qaT" else \
                                dst_d[hd // 4][(hd % 4) * 128:(hd % 4 + 1) * 128, :]
                            P.add("sp", lambda e, sqb=sqb, t=t, dd=dd: e.dma_start(
                                out=dd[:, T0 + t * 512:T0 + (t + 1) * 512], in_=sqb),
                                r=[("stg", s_sq)], w=[(nm, hd, hp, t)], dma=True)
            for (c0, dst_d, nm, fn_) in (((C_VG, vg_d, "vg", AF.Copy), (C_RG, rg_d, "rg", AF.Silu),
                                         (C_VA, va_d, "va", AF.Copy)) if "no_wt" not in PHASES else ()):
                for q4 in range(4):
                    b = load_w256(win_d, c0 + q4 * 256)
                    for i in range(8):
                        j = next_ps()
                        for k in range(16):
                            P.add("pe", lambda e, k=k, j=j, i=i, b=b: e.matmul(
                                ps[j][:, 0:256], lhsT=hT[:, k, i * 128:(i + 1) * 128], rhs=W256[b][:, k, :],
                                start=(k == 0), stop=(k == 15)),
                                r=[("W", b), ("hT", i)], w=[("ps", j)])
                        s = next_stg()
                        sv = stg[s][:, 0:256]
                        P.add("act", lambda e, j=j, sv=sv, fn_=fn_: e.activation(out=sv, in_=ps[j][:, 0:256], func=fn_),
                              r=[("ps", j)], w=[("stg", s)])
                        dd = dst_d[hp][i * 128:(i + 1) * 128, :] if nm == "va" else \
                            dst_d[T0 + i * 128:T0 + (i + 1) * 128, :]
                        P.add("sp", lambda e, sv=sv, q4=q4, dd=dd: e.dma_start(
                            out=dd[:, q4 * 256:(q4 + 1) * 256], in_=sv),
                            r=[("stg", s)], w=[(nm, hp, i, q4)], dma=True)

        def all_keys(nm, *dims):
            import itertools
            return [(nm,) + t for t in itertools.product(*[range(d) for d in dims])]

        def phaseB():
            B = Arena(arena_t, AB)
            fac = []
            for hd in range(4):
                fac.append(dict(qq=B.alloc([128, NT], BF16), kk=B.alloc([128, NT], BF16),
                                qi=B.alloc([128, NT], BF16), kst=B.alloc([128, 16, 128], BF16),
                                dec=B.alloc([128, 16], F32)))
            Sf = B.alloc([128, 256], F32)
            Sb = B.alloc([128, 256], BF16)
            Sin = B.alloc([128, 256], F32)
            mark = B.off
            for hg in range(2):
                P.add("pool", lambda e, hg=hg: e.collective_compute(
                    "AllGather", ALU.bypass, replica_groups=RGROUPS,
                    ins=[kaT_d[hg].opt()], outs=[kaT_g[hg].opt()]),
                    r=all_keys("kaT", 8, 2, 2), w=[("kaT_g", hg)], kind="cc")
            for hv in range(2):
                P.add("pool", lambda e, hv=hv: e.collective_compute(
                    "AllGather", ALU.bypass, replica_groups=RGROUPS,
                    ins=[va_d[hv].opt()], outs=[va_g[hv].opt()]),
                    r=all_keys("va", 2, 8, 4), w=[("va_g", hv)], kind="cc")
            p1 = [dict(qf=B.alloc([128, NT], F32), kf=B.alloc([128, NT], F32), d1=B.alloc([128, NT], F32),
                       vsb=B.alloc([128, 16, 256], BF16)) for _ in range(2)]
            bb, d4 = B.alloc([128, NT], F32), B.alloc([128, NT], F32)
            exs = [B.alloc([128, NT], F32) for _ in range(2)]
            rmask = B.alloc([128, NT], F32)
            P.add("dve", lambda e: e.memset(rmask, 1.0), w=[("rmask",)])
            P.add("dve", lambda e: e.memset(rmask[:, 0:NT:128], 0.0), w=[("rmask",)])
            SC = 128 ** -0.5

            def p1_loads(hd):
                si = hd % 2
                S_ = p1[si]
                P.add("sp", lambda e: e.dma_start(out=S_["qf"], in_=qgT_d[hd * 128:(hd + 1) * 128, :]),
                      r=all_keys("qgT", 4, 2, 2), w=[("qf", si)], dma=True)
                P.add("sp", lambda e: e.dma_start(out=S_["kf"], in_=kgT_d[hd * 128:(hd + 1) * 128, :]),
                      r=all_keys("kgT", 4, 2, 2), w=[("kf", si)], dma=True)
                P.add("sp", lambda e: e.dma_start(out=S_["d1"], in_=laT_d[hd * 128:(hd + 1) * 128, :]),
                      r=all_keys("laT", 4, 2, 2), w=[("d1", si)], dma=True)
                P.add("sp", lambda e: e.dma_start(
                    out=S_["vsb"], in_=vg_d[:, hd * 256:(hd + 1) * 256].rearrange("(i p) c -> p i c", p=128)),
                    r=all_keys("vg", 2, 8, 4), w=[("vsb1", si)], dma=True)

            def p1_compute(hd):
                si = hd % 2
                S_ = p1[si]
                f = fac[hd]
                qf, kf, d1, vsb = S_["qf"], S_["kf"], S_["d1"], S_["vsb"]
                P.add("act", lambda e: e.activation(out=d1, in_=d1, func=AF.Identity, scale=-1.0 / 16.0),
                      r=[("d1", si)], w=[("d1", si)])
                P.add("dve", lambda e: e.tensor_tensor_scan(out=bb, data0=rmask, data1=d1, initial=0.0,
                                                            op0=ALU.mult, op1=ALU.add),
                      r=[("d1", si), ("rmask",)], w=[("bb",)])
                b3 = bb.rearrange("p (i t) -> p i t", i=16)
                d13 = d1.rearrange("p (i t) -> p i t", i=16)
                d43 = d4.rearrange("p (i t) -> p i t", i=16)
                bref = b3[:, :, 63:64].broadcast_to([128, 16, 128])
                blast = b3[:, :, 127:128].broadcast_to([128, 16, 128])
                P.add("dve", lambda e: e.tensor_tensor(out=d13, in0=b3, in1=bref, op=ALU.subtract),
                      r=[("bb",)], w=[("d1", si)])
                P.add("dve", lambda e: e.tensor_tensor(out=d43, in0=blast, in1=b3, op=ALU.subtract),
                      r=[("bb",)], w=[("d4",)])
                P.add("act", lambda e: e.activation(out=exs[0], in_=d1, func=AF.Exp), r=[("d1", si)], w=[("ex", 0)])
                P.add("act", lambda e: e.activation(out=exs[1], in_=d1, func=AF.Exp, scale=-1.0),
                      r=[("d1", si)], w=[("ex", 1)])
                P.add("dve", lambda e: e.scalar_tensor_tensor(out=f["qq"], in0=qf, scalar=SC, in1=exs[0],
                                                              op0=ALU.mult, op1=ALU.mult),
                      r=[("qf", si), ("ex", 0)], w=[("fac", hd, "qq")])
                P.add("dve", lambda e: e.tensor_tensor(out=f["kk"], in0=kf, in1=exs[1], op=ALU.mult),
                      r=[("kf", si), ("ex", 1)], w=[("fac", hd, "kk")])
                P.add("act", lambda e: e.activation(out=exs[0], in_=bb, func=AF.Exp), r=[("bb",)], w=[("ex", 0)])
                P.add("act", lambda e: e.activation(out=f["dec"], in_=b3[:, :, 127], func=AF.Exp),
                      r=[("bb",)], w=[("fac", hd, "dec")])
                P.add("act", lambda e: e.activation(out=exs[1], in_=d4, func=AF.Exp), r=[("d4",)], w=[("ex", 1)])
                P.add("dve", lambda e: e.scalar_tensor_tensor(out=f["qi"], in0=qf, scalar=SC, in1=exs[0],
                                                              op0=ALU.mult, op1=ALU.mult),
                      r=[("qf", si), ("ex", 0)], w=[("fac", hd, "qi")])
                ksb = d4.bitcast(BF16)[:, 0:NT]
                P.add("dve", lambda e: e.tensor_tensor(out=ksb, in0=kf, in1=exs[1], op=ALU.mult),
                      r=[("kf", si), ("ex", 1)], w=[("d4",)])
                for half in range(2):
                    j = next_ps()
                    pb = ps[j][:].bitcast(BF16)
                    for ii in range(8):
                        i = half * 8 + ii
                        P.add("pe", lambda e, i=i, ii=ii, pb=pb: e.transpose(
                            out=pb[:, ii * 128:(ii + 1) * 128], in_=ksb[:, i * 128:(i + 1) * 128],
                            identity=ident[:]),
                            r=[("d4",), ("ident",)], w=[("ps", j)])
                    P.add("act", lambda e, pb=pb, half=half: e.activation(
                        out=f["kst"][:, half * 8:(half + 1) * 8, :],
                        in_=pb.rearrange("p (i d) -> p i d", i=8), func=AF.Copy),
                        r=[("ps", j)], w=[("fac", hd, "kst")])
                P.add("dve", lambda e: e.memset(Sf, 0.0), w=[("Sf",)])
                for i in range(16):
                    j = next_ps()
                    P.add("pe", lambda e, j=j, i=i: e.matmul(ps[j][:, 0:256], lhsT=f["kst"][:, i, :],
                                                             rhs=vsb[:, i, :], start=True, stop=True),
                          r=[("fac", hd, "kst"), ("vsb1", si)], w=[("ps", j)])
                    P.add("dve", lambda e, j=j, i=i: e.scalar_tensor_tensor(
                        out=Sf, in0=Sf, scalar=f["dec"][:, i:i + 1], in1=ps[j][:, 0:256],
                        op0=ALU.mult, op1=ALU.add),
                        r=[("ps", j), ("Sf",), ("fac", hd, "dec")], w=[("Sf",)])
                P.add("sp", lambda e: e.dma_start(out=st_d[hd * 128:(hd + 1) * 128, :], in_=Sf),
                      r=[("Sf",)], w=[("st_d", hd)], dma=True)

            p1_loads(0)
            for hd in range(4):
                if hd + 1 < 4:
                    p1_loads(hd + 1)
                p1_compute(hd)
            if "noBx" in PHASES:
                return
            P.add("pool", lambda e: e.collective_compute(
                "AllGather", ALU.bypass, replica_groups=RGROUPS,
                ins=[st_d.opt()], outs=[st_g.opt()]),
                r=[("st_d", h_) for h_ in range(4)], w=[("st_g",)], kind="cc")
            barrier()
            if "noBa" in PHASES:
                return
            B.off = mark
            sets = []
            for s_ in range(2):
                sets.append(dict(
                    qT=B.alloc([128, NT], BF16), kT=B.alloc([128, NT], BF16), kTp=B.alloc([128, NT], BF16),
                    Vr={1: B.alloc([128, 16, 128], BF16), 4: B.alloc([128, 16, 128], BF16),
                        16: B.alloc([128, 16, 128], BF16)},
                    Vp={1: B.alloc([128, 1, 128], BF16), 4: B.alloc([128, 4, 128], BF16),
                        16: B.alloc([128, 16, 128], BF16)}))
            accs = [B.alloc([128, 2, NT], F32) for _ in range(2)]
            PT = [B.alloc([128, 512], BF16) for _ in range(3)]
            oTb = [B.alloc([128, NT], BF16) for _ in range(2)]
            SCA = 128 ** -0.5

            def att_loads(hd):
                si = hd % 2
                S_ = sets[si]
                P.add("sp", lambda e: e.dma_start(out=S_["qT"], in_=qaT_d[hd * 128:(hd + 1) * 128, :]),
                      r=all_keys("qaT", 8, 2, 2), w=[("qT", si)], dma=True)
                P.add("sp", lambda e: e.dma_start(
                    out=S_["kT"], in_=kaT_d[hd // 4][(hd % 4) * 128:(hd % 4 + 1) * 128, :]),
                    r=all_keys("kaT", 8, 2, 2), w=[("kT", si)], dma=True)
                P.add("sp", lambda e: e.dma_start(
                    out=S_["kTp"], in_=kaT_g[hd // 4][(hd % 4) * 128:(hd % 4 + 1) * 128, :]),
                    r=[("kaT_g", hd // 4)], w=[("kTp", si)], dma=True)
                hc = slice(hd * 128, (hd + 1) * 128)
                for r_ in (1, 4, 16):
                    NL = 16 // r_
                    for hv in range(2):
                        for (srcT, dstT, rk, wk, prev) in (
                                (va_d[hv], S_["Vr"][r_], all_keys("va", 2, 8, 4), ("Vr", r_, si), False),
                                (va_g[hv][0:HP, :], S_["Vp"][r_], [("va_g", hv)], ("Vp", r_, si), True)):
                            src = srcT[:, hc]
                            if r_ == 16:
                                src = src.rearrange("(j r) c -> j r c", r=16)
                                dst = dstT[hv * 64:(hv + 1) * 64, :, :]
                            elif r_ == 1:
                                src = src.rearrange("(n j) c -> j n c", j=128)
                                if prev:
                                    if hv == 0:
                                        continue
                                    src = src[:, 7:8, :]
                                    dst = dstT[:, 0:1, :]
                                else:
                                    dst = dstT[:, hv * 8:(hv + 1) * 8, :]
                            else:
                                NLh = NL // 2
                                src4 = src.rearrange("(n j r) c -> n j r c", j=128, r=r_)
                                dst4 = dstT if prev else dstT.rearrange("p (r n) c -> p r n c", r=r_)
                                if prev:
                                    if hv == 0:
                                        continue
                                    P.add("sp", lambda e, src=src4[NLh - 1], dst=dst4[:, :, :]: e.dma_start(
                                        out=dst, in_=src), r=rk, w=[wk], dma=True)
                                else:
                                    for n in range(NLh):
                                        P.add("sp", lambda e, src=src4[n], dst=dst4[:, :, hv * NLh + n, :]:
                                              e.dma_start(out=dst, in_=src), r=rk, w=[wk], dma=True)
                                continue
                            P.add("sp", lambda e, src=src, dst=dst: e.dma_start(out=dst, in_=src),
                                  r=rk, w=[wk], dma=True)

            def att_compute(hd):
                si = hd % 2
                S_ = sets[si]
                qT, kT, kTp, Vr, Vp = S_["qT"], S_["kT"], S_["kTp"], S_["Vr"], S_["Vp"]
                acc = accs[si]
                G = []
                for r_ in (1, 4, 16):
                    NL = 16 // r_
                    blocks = [(res, nl) for res in range(r_) for nl in range(NL)]
                    for g0 in range(0, 16, 2):
                        pair = blocks[g0:g0 + 2]
                        first = [nl == 0 for (_, nl) in pair]
                        mv = 1 if not (first[0] or first[1]) else (2 if (first[0] and first[1]) else 0)
                        sl = []
                        for (res, nl) in pair:
                            q0 = res + r_ * 128 * nl
                            qs = slice(q0, q0 + r_ * 127 + 1, r_)
                            if nl == 0:
                                p0 = res + r_ * 128 * (NL - 1)
                                kprev = kTp[:, p0:p0 + r_ * 127 + 1:r_]
                                vprev = Vp[r_][:, res, :]
                                kpk, vpk = ("kTp", si), ("Vp", r_, si)
                            else:
                                p0 = res + r_ * 128 * (nl - 1)
                                kprev = kT[:, p0:p0 + r_ * 127 + 1:r_]
                                vprev = Vr[r_][:, res * NL + nl - 1, :]
                                kpk, vpk = ("kT", si), ("Vr", r_, si)
                            vcur = Vr[r_][:, res * NL + nl, :]
                            sl.append((qs, kprev, kpk, vprev, vpk, vcur))
                        G.append(dict(r_=r_, mv=mv, sl=sl))
                NG = len(G)

                def bank(g):
                    gg = hd * NG + g
                    return gg % 3, 3 + gg % 3, PT[gg % 3]

                def S(g):
                    jS, jO, pt = bank(g)
                    mv = G[g]["mv"]
                    P.add("pe", lambda e: e.matmul(ps[jS][:], lhsT=ident[:], rhs=m4[:, mv, :],
                                                   start=True, stop=False),
                          r=[("ident",), ("m4",)], w=[("ps", jS)])
                    for bi, (qs, kprev, kpk, vprev, vpk, vcur) in enumerate(G[g]["sl"]):
                        last = (bi == 1)
                        P.add("pe", lambda e, bi=bi, kprev=kprev, qs=qs: e.matmul(
                            ps[jS][:, (2 * bi) * 128:(2 * bi + 1) * 128], lhsT=kprev, rhs=qT[:, qs],
                            start=False, stop=False),
                            r=[kpk, ("qT", si)], w=[("ps", jS)])
                        P.add("pe", lambda e, bi=bi, qs=qs, last=last: e.matmul(
                            ps[jS][:, (2 * bi + 1) * 128:(2 * bi + 2) * 128], lhsT=kT[:, qs], rhs=qT[:, qs],
                            start=False, stop=last),
                            r=[("kT", si), ("qT", si)], w=[("ps", jS)])

                def E(g):
                    jS, jO, pt = bank(g)
                    P.add("act", lambda e: e.activation(out=pt, in_=ps[jS][:], func=AF.Exp, scale=SCA),
                          r=[("ps", jS)], w=[("PT", jS)])

                def V(g):
                    jS, jO, pt = bank(g)
                    r_ = G[g]["r_"]
                    po = ps[jO][:].rearrange("p (b a q) -> p b a q", b=2, a=2)
                    for bi, (qs, kprev, kpk, vprev, vpk, vcur) in enumerate(G[g]["sl"]):
                        P.add("pe", lambda e, bi=bi, vprev=vprev: e.matmul(
                            po[:, bi, 0, :], lhsT=vprev, rhs=pt[:, (2 * bi) * 128:(2 * bi + 1) * 128],
                            start=True, stop=False),
                            r=[vpk, ("PT", jS)], w=[("ps", jO)])
                        P.add("pe", lambda e, bi=bi, vcur=vcur: e.matmul(
                            po[:, bi, 0, :], lhsT=vcur, rhs=pt[:, (2 * bi + 1) * 128:(2 * bi + 2) * 128],
                            start=False, stop=True),
                            r=[("Vr", r_, si), ("PT", jS)], w=[("ps", jO)])
                        P.add("pe", lambda e, bi=bi: e.matmul(
                            po[:, bi, 1, :], lhsT=ones[:], rhs=pt[:, (2 * bi) * 128:(2 * bi + 1) * 128],
                            start=True, stop=False),
                            r=[("ones",), ("PT", jS)], w=[("ps", jO)])
                        P.add("pe", lambda e, bi=bi: e.matmul(
                            po[:, bi, 1, :], lhsT=ones[:], rhs=pt[:, (2 * bi + 1) * 128:(2 * bi + 2) * 128],
                            start=False, stop=True),
                            r=[("ones",), ("PT", jS)], w=[("ps", jO)])

                def Dv(g):
                    jS, jO, pt = bank(g)
                    r_ = G[g]["r_"]
                    po = ps[jO][:].rearrange("p (b a q) -> p b a q", b=2, a=2)
                    for bi, (qs, kprev, kpk, vprev, vpk, vcur) in enumerate(G[g]["sl"]):
                        if r_ == 1:
                            P.add("act", lambda e, bi=bi, qs=qs: e.activation(
                                out=acc[:, :, qs], in_=po[:, bi, :, :], func=AF.Copy),
                                r=[("ps", jO)], w=[("acc", si)])
                        else:
                            P.add("dve", lambda e, bi=bi, qs=qs: e.tensor_tensor(
                                out=acc[:, :, qs], in0=po[:, bi, :, :], in1=acc[:, :, qs], op=ALU.add),
                                r=[("ps", jO), ("acc", si)], w=[("acc", si)])

                S(0)
                S(1)
                for g in range(NG):
                    E(g)
                    V(g)
                    if g + 2 < NG:
                        S(g + 2)
                    Dv(g)
                ob_ = oTb[si]
                P.add("dve", lambda e: e.reciprocal(out=acc[:, 1, :], in_=acc[:, 1, :]),
                      r=[("acc", si)], w=[("acc", si)])
                P.add("dve", lambda e: e.tensor_tensor(out=ob_, in0=acc[:, 0, :], in1=acc[:, 1, :], op=ALU.mult),
                      r=[("acc", si)], w=[("oTb", si)])
                P.add("sp", lambda e: e.dma_start(out=oT_d[1024 + hd * 128:1024 + (hd + 1) * 128, :], in_=ob_),
                      r=[("oTb", si)], w=[("oT_d", 8 + hd)], dma=True)

            att_loads(0)
            for hd in range(8):
                if hd + 1 < 8:
                    att_loads(hd + 1)
                att_compute(hd)
            barrier()
            if "noB2" in PHASES:
                return
            B.off = mark
            g2 = []
            for s_ in range(2):
                g2.append(dict(vsb=B.alloc([128, 16, 256], BF16), rsb=B.alloc([128, 16, 256], BF16),
                               Sball=B.alloc([128, 16, 256], BF16), oTg=B.alloc([128, 2, NT], BF16),
                               sc=B.alloc([128, 48], F32), Sin=B.alloc([128, 256], F32)))
            ot = [B.alloc([128, 256], F32) for _ in range(4)]
            ob = [B.alloc([128, 256], BF16) for _ in range(4)]
            Sm = [B.alloc([128, 128], BF16) for _ in range(2)]
            junk = B.alloc([128, 256], BF16)

            def g2_loads(hd):
                si = hd % 2
                S_ = g2[si]
                P.add("sp", lambda e: e.dma_start(
                    out=S_["vsb"], in_=vg_d[:, hd * 256:(hd + 1) * 256].rearrange("(i p) c -> p i c", p=128)),
                    r=all_keys("vg", 2, 8, 4), w=[("vsb", si)], dma=True)
                P.add("sp", lambda e: e.dma_start(
                    out=S_["rsb"], in_=rg_d[:, hd * 256:(hd + 1) * 256].rearrange("(i p) c -> p i c", p=128)),
                    r=all_keys("rg", 2, 8, 4), w=[("rsb", si)], dma=True)
                P.add("sp", lambda e: e.dma_start(out=S_["Sin"], in_=st_g[hd * 128:(hd + 1) * 128, :]),
                      r=[("st_g",)], w=[("Sin", si)], dma=True)

            def g2_compute(hd):
                si = hd % 2
                S_ = g2[si]
                f = fac[hd]
                vsb_, rsb_, Sball, oTg, sc = S_["vsb"], S_["rsb"], S_["Sball"], S_["oTg"], S_["sc"]
                P.add("dve", lambda e: e.tensor_scalar(out=Sf, in0=S_["Sin"], scalar1=flag, scalar2=None,
                                                       op0=ALU.mult),
                      r=[("Sin", si), ("small", "flag")], w=[("Sf",)])
                P.add("act", lambda e: e.activation(out=Sball[:, 0, :], in_=Sf, func=AF.Copy),
                      r=[("Sf",)], w=[("Sball", si, 0)])
                P.add("dve", lambda e: e.memset(sc[:, 0:16], 0.0), w=[("sc", si, "ss")])
                for i in range(15):
                    jU = 6 + i % 2
                    P.add("pe", lambda e, jU=jU, i=i: e.matmul(ps[jU][:, 0:256], lhsT=f["kst"][:, i, :],
                                                               rhs=vsb_[:, i, :], start=True, stop=True),
                          r=[("fac", hd, "kst"), ("vsb", si)], w=[("ps", jU)])
                    P.add("dve", lambda e, jU=jU, i=i: e.scalar_tensor_tensor(
                        out=Sf, in0=Sf, scalar=f["dec"][:, i:i + 1], in1=ps[jU][:, 0:256],
                        op0=ALU.mult, op1=ALU.add),
                        r=[("ps", jU), ("Sf",), ("fac", hd, "dec")], w=[("Sf",)])
                    P.add("act", lambda e, i=i: e.activation(out=Sball[:, i + 1, :], in_=Sf, func=AF.Copy),
                          r=[("Sf",)], w=[("Sball", si, i + 1)])

                def St(i):
                    tk = slice(i * 128, (i + 1) * 128)
                    jA = i % 2
                    sm = Sm[i % 2]
                    P.add("pe", lambda e: e.matmul(ps[jA][:, 0:128], lhsT=f["kk"][:, tk], rhs=f["qq"][:, tk],
                                                   start=True, stop=True),
                          r=[("fac", hd, "kk"), ("fac", hd, "qq")], w=[("ps", jA)])
                    P.add("dve", lambda e: e.tensor_tensor(out=sm, in0=ps[jA][:, 0:128], in1=mgla[:], op=ALU.mult),
                          r=[("ps", jA), ("mgla",)], w=[("Sm", i % 2)])

                def Oc(i):
                    tk = slice(i * 128, (i + 1) * 128)
                    jO_ = 2 + i % 3
                    sm = Sm[i % 2]
                    q4 = i % 4
                    P.add("pe", lambda e: e.matmul(ps[jO_][:, 0:256], lhsT=sm, rhs=vsb_[:, i, :],
                                                   start=True, stop=False),
                          r=[("Sm", i % 2), ("vsb", si)], w=[("ps", jO_)])
                    P.add("pe", lambda e: e.matmul(ps[jO_][:, 0:256], lhsT=f["qi"][:, tk], rhs=Sball[:, i, :],
                                                   start=False, stop=True),
                          r=[("fac", hd, "qi"), ("Sball", si, i)], w=[("ps", jO_)])
                    P.add("act", lambda e: e.activation(out=junk, in_=ps[jO_][:, 0:256], func=AF.Square,
                                                        accum_out=sc[:, i:i + 1]),
                          r=[("ps", jO_), ("sc", si, "ss")], w=[("junk",), ("sc", si, "ss", i)])
                    P.add("act", lambda e: e.activation(out=sc[:, 16 + i:17 + i], in_=sc[:, i:i + 1], func=AF.Sqrt,
                                                        bias=eps_t, scale=1.0 / 256),
                          r=[("sc", si, "ss", i), ("small", "eps")], w=[("sc", si, "ss2", i)])
                    P.add("dve", lambda e: e.reciprocal(out=sc[:, 32 + i:33 + i], in_=sc[:, 16 + i:17 + i]),
                          r=[("sc", si, "ss2", i)], w=[("sc", si, "rs", i)])
                    P.add("dve", lambda e: e.scalar_tensor_tensor(out=ot[q4], in0=ps[jO_][:, 0:256],
                                                                  scalar=sc[:, 32 + i:33 + i], in1=goutb[:],
                                                                  op0=ALU.mult, op1=ALU.mult),
                          r=[("ps", jO_), ("sc", si, "rs", i), ("goutb",)], w=[("ot", q4)])
                    P.add("dve", lambda e: e.tensor_tensor(out=ob[q4], in0=ot[q4], in1=rsb_[:, i, :], op=ALU.mult),
                          r=[("ot", q4), ("rsb", si)], w=[("ob", q4)])

                def Tr(i):
                    tk = slice(i * 128, (i + 1) * 128)
                    q4 = i % 4
                    pb = ps[5][:].bitcast(BF16)
                    for c in range(2):
                        P.add("pe", lambda e, c=c: e.transpose(out=pb[:, c * 128:(c + 1) * 128],
                                                               in_=ob[q4][:, c * 128:(c + 1) * 128],
                                                               identity=ident[:]),
                              r=[("ob", q4), ("ident",)], w=[("ps", 5)])
                    P.add("act", lambda e: e.activation(
                        out=oTg[:, :, tk], in_=pb[:, 0:256].rearrange("p (c t) -> p c t", c=2), func=AF.Copy),
                        r=[("ps", 5)], w=[("oTg", si)])

                for s_ in range(16 + 3):
                    if s_ < 16:
                        St(s_)
                    if 0 <= s_ - 1 < 16:
                        Oc(s_ - 1)
                    if 0 <= s_ - 3 < 16:
                        Tr(s_ - 3)
                for c in range(2):
                    P.add("sp", lambda e, c=c: e.dma_start(
                        out=oT_d[hd * 256 + c * 128:hd * 256 + (c + 1) * 128, :], in_=oTg[:, c, :]),
                        r=[("oTg", si)], w=[("oT_d", hd * 2 + c)], dma=True)

            g2_loads(0)
            for hd in range(4):
                if hd + 1 < 4:
                    g2_loads(hd + 1)
                g2_compute(hd)
            barrier()

        def phaseC(hp):
            T0 = hp * HP
            for k4 in range(4):
                P.add("sp", lambda e, k4=k4: e.dma_start(
                    out=hT[:, k4 * 4:(k4 + 1) * 4, :],
                    in_=oT_d[k4 * 512:(k4 + 1) * 512, T0:T0 + HP].rearrange("(k p) t -> p k t", p=128)),
                    r=[("oT_d", j) for j in range(16)], w=[("hT", i) for i in range(8)], dma=True)
            for i in range(8):
                P.add("sp", lambda e, i=i: e.dma_start(out=xs[i], in_=x1_d[T0 + i * 128:T0 + (i + 1) * 128, :]),
                      r=[("x1_d", hp, i)], w=xk(i), dma=True)
            for dgp in range(8):
                b = load_w256(wout_d, dgp * 256)
                for i in range(8):
                    j = next_ps()
                    for k in range(16):
                        P.add("pe", lambda e, k=k, j=j, i=i, b=b: e.matmul(
                            ps[j][:, 0:256], lhsT=hT[:, k, i * 128:(i + 1) * 128], rhs=W256[b][:, k, :],
                            start=(k == 0), stop=(k == 15)),
                            r=[("W", b), ("hT", i)], w=[("ps", j)])
                    dgk = dgp // 2
                    P.add("dve", lambda e, j=j, i=i, dgp=dgp: e.tensor_tensor(
                        out=xs[i][:, dgp * 256:(dgp + 1) * 256], in0=ps[j][:, 0:256],
                        in1=xs[i][:, dgp * 256:(dgp + 1) * 256], op=ALU.add),
                        r=[("ps", j), ("xs", i, dgk)], w=[("xs", i, dgk)])
            load_gain(n2_d)
            norm_all()
            ffn(w2g, w2u, w2d)
            for i in range(8):
                P.add("sp", lambda e, i=i: e.dma_start(out=out_d[T0 + i * 128:T0 + (i + 1) * 128, :], in_=xs[i]),
                      r=xk(i), w=[("out", hp, i)], dma=True)

        wnobar[0] = True
        if "A0" in PHASES:
            phaseA(0)
        if "A1" in PHASES:
            phaseA(1)
        wnobar[0] = False
        if "B" in PHASES:
            phaseB()
        if "C0" in PHASES:
            phaseC(0)
        if "C1" in PHASES:
            phaseC(1)

        P.schedule()
        P.emit(nc)
    return nc


_NC_CACHE = {}


def _host_consts(h):
    half = 64
    inv = 1.0 / (10000.0 ** (np.arange(half, dtype=np.float32) / half))
    pos = (h * NT + np.arange(NT)).astype(np.float32)
    ang = pos[None, :] * inv[:, None]
    cosT = np.concatenate([np.cos(ang), np.cos(ang)], 0).astype(np.float32)
    sinT = np.concatenate([-np.sin(ang), np.sin(ang)], 0).astype(np.float32)
    j = np.arange(128)[:, None]
    i = np.arange(128)[None, :]
    mcur = np.where(j <= i, 0.0, NEG).astype(np.float32)
    mprev = np.where(j >= i, 0.0, NEG).astype(np.float32)
    mprevh = mprev if h == 1 else np.full((128, 128), NEG, np.float32)
    m4 = np.stack([np.concatenate([mprevh, mcur, mprev, mcur], 1),
                   np.concatenate([mprev, mcur, mprev, mcur], 1),
                   np.concatenate([mprevh, mcur, mprevh, mcur], 1)], 0).astype(np.float32)
    mgla = (j <= i).astype(np.float32)
    ident = np.eye(128, dtype=np.float32)
    swapT = np.zeros((128, 128), np.float32)
    swapT[(np.arange(128) + 64) % 128, np.arange(128)] = 1.0
    flag = np.full((128, 1), float(h), np.float32)
    return dict(cosT=cosT, sinT=sinT, m4=m4, mgla=mgla, ident=ident, swapT=swapT, flag=flag)


def kernel(x, ffn1_norm, ffn1_w_gate, ffn1_w_up, ffn1_w_down, mix_norm, w_in,
           gla_gate_up, gla_gate_bias, gla_out_norm, att_q_norm, att_k_norm, w_out,
           ffn2_norm, ffn2_w_gate, ffn2_w_up, ffn2_w_down):
    f = lambda a: np.ascontiguousarray(np.asarray(a, dtype=np.float32))
    x = f(x)
    common = dict(
        w1g=f(ffn1_w_gate[0]), w1u=f(ffn1_w_up[0]), w1d=f(ffn1_w_down[0]),
        w2g=f(ffn2_w_gate[0]), w2u=f(ffn2_w_up[0]), w2d=f(ffn2_w_down[0]),
        win=f(w_in[0]), wout=f(w_out[0]),
        n1=f(ffn1_norm[0]).reshape(1, D), nm=f(mix_norm[0]).reshape(1, D), n2=f(ffn2_norm[0]).reshape(1, D),
        gout=f(gla_out_norm[0]).reshape(1, 256),
        gq=f(att_q_norm[0]).reshape(128, 1), gk=f(att_k_norm[0]).reshape(128, 1),
        gup=f(gla_gate_up[0]),
        gbias=f(np.asarray(gla_gate_bias[0]).reshape(4, 128).T),
    )
    consts = [_host_consts(0), _host_consts(1)]
    in_maps = []
    for c in range(8):
        b, h = c // 2, c % 2
        m = dict(common)
        m["x"] = np.ascontiguousarray(x[b, h * NT:(h + 1) * NT, :])
        m.update(consts[h])
        in_maps.append(m)
    if "nc" not in _NC_CACHE:
        _NC_CACHE["nc"] = build_nc()
    nc = _NC_CACHE["nc"]
    res = run_bass_kernel_spmd(nc, in_maps, core_ids=list(range(8)))
    out = np.empty((4, 4096, D), np.float32)
    for c in range(8):
        b, h = c // 2, c % 2
        out[b, h * NT:(h + 1) * NT, :] = res.results[c]["out"]
    return out
```

```python
import numpy as np
import concourse.bass as bass
import concourse.mybir as mybir
from concourse.bass_utils import run_bass_kernel_spmd

F32 = mybir.dt.float32
BF16 = mybir.dt.bfloat16
AF = mybir.ActivationFunctionType
ALU = mybir.AluOpType

D = 2048
DFF = 5632
DIN = 6160
NT = 2048
HP = 1024
EPS = 1e-6
NEG = -30000.0
NR = 8
PHASES = "A0 A1 B C0 C1"
DEBUG_OUT = set()
RGROUPS = [[0, 1], [2, 3], [4, 5], [6, 7]]
C_QG, C_KG, C_VG, C_GL, C_RG, C_QA, C_KA, C_VA = 0, 512, 1024, 2048, 2064, 3088, 4112, 5136


class Prog:
    def __init__(self):
        self.ops = []
        self.bar = None

    def add(self, eng, fn, r=(), w=(), dma=False, kind=None, nobar=False):
        self.ops.append(dict(eng=eng, fn=fn, r=list(r), w=list(w), dma=dma, kind=kind,
                             bar=(None if nobar else self.bar), signal=False))
        return len(self.ops) - 1

    def barrier(self, fn):
        i = self.add("dve", fn, kind="barrier")
        self.bar = i

    def schedule(self):
        ops = self.ops
        last_w, readers = {}, {}
        last_eng = {}
        dma_hist = {"sp": [], "pool": []}
        for i, op in enumerate(ops):
            deps = set()
            if op["kind"] == "barrier":
                for e, j in last_eng.items():
                    deps.add(j)
                for e in dma_hist:
                    deps.update(dma_hist[e][-NR:])
            else:
                for k in op["r"]:
                    if k in last_w:
                        deps.add(last_w[k])
                for k in op["w"]:
                    if k in last_w:
                        deps.add(last_w[k])
                    deps.update(readers.get(k, ()))
                if op["bar"] is not None:
                    deps.add(op["bar"])
            deps.discard(i)
            for k in op["r"]:
                readers.setdefault(k, []).append(i)
            for k in op["w"]:
                last_w[k] = i
                readers[k] = []
            dl = []
            for j in deps:
                pj = ops[j]
                if (not pj["dma"]) and (not op["dma"]) and pj["eng"] == "pe" and op["eng"] == "pe" \
                        and pj["kind"] != "cc":
                    continue
                dl.append(j)
            op["deps"] = dl
            if op["dma"]:
                dma_hist[op["eng"]].append(i)
            else:
                last_eng[op["eng"]] = i
        for op in ops:
            for j in op["deps"]:
                if not ops[j]["dma"]:
                    ops[j]["signal"] = True
        cnt = {}
        dcnt = {"sp": 0, "pool": 0}
        ncc = 0
        for op in ops:
            if op["dma"]:
                idx = dcnt[op["eng"]]
                dcnt[op["eng"]] += 1
                op["sem"] = ("ring", op["eng"], idx % NR)
                op["val"] = 16 * (idx // NR + 1)
            elif op["kind"] == "cc":
                op["sem"] = ("cc", ncc)
                op["val"] = 1
                ncc += 1
            elif op["signal"]:
                cnt[op["eng"]] = cnt.get(op["eng"], 0) + 1
                op["sem"] = ("eng", op["eng"])
                op["val"] = cnt[op["eng"]]
        self.ncc = ncc
        for op in ops:
            waits = {}
            for j in op["deps"]:
                pj = ops[j]
                s, v = pj["sem"], pj["val"]
                if waits.get(s, 0) < v:
                    waits[s] = v
            if op["dma"] and op["val"] > 16:
                s = op["sem"]
                waits[s] = max(waits.get(s, 0), op["val"] - 16)
            op["waits"] = waits
        self.final = {}
        for op in ops:
            if op["dma"]:
                self.final[op["sem"]] = op["val"]

    def emit(self, nc):
        ops = self.ops
        sems = {}
        import contextlib
        with contextlib.ExitStack() as st:
            for e in ("pe", "act", "dve"):
                sems[("eng", e)] = st.enter_context(nc.semaphore("s_" + e))
            for e in ("sp", "pool"):
                for k in range(NR):
                    sems[("ring", e, k)] = st.enter_context(nc.semaphore("r_%s%d" % (e, k)))
            for k in range(self.ncc):
                sems[("cc", k)] = st.enter_context(nc.semaphore("cc%d" % k))
            block = st.enter_context(nc.Block())

            def run(name, e):
                waited = {}
                for op in ops:
                    if op["eng"] != name:
                        continue
                    for s, v in op["waits"].items():
                        if waited.get(s, 0) < v:
                            e.wait_ge(sems[s], v)
                            waited[s] = v
                    ins = op["fn"](e)
                    if op["dma"]:
                        ins.then_inc(sems[op["sem"]], 16)
                    elif op["kind"] == "cc":
                        ins.then_inc(sems[op["sem"]], 1)
                    elif op["signal"]:
                        ins.then_inc(sems[op["sem"]], 1)
                if name in ("sp", "pool"):
                    for s, v in self.final.items():
                        if s[1] == name and waited.get(s, 0) < v:
                            e.wait_ge(sems[s], v)

            @block.tensor
            def _(e):
                run("pe", e)

            @block.scalar
            def _(e):
                run("act", e)

            @block.vector
            def _(e):
                run("dve", e)

            @block.gpsimd
            def _(e):
                run("pool", e)

            @block.sync
            def _(e):
                run("sp", e)


class Arena:
    def __init__(self, base_ap, nbytes):
        self.base = base_ap
        self.n = nbytes
        self.off = 0

    def alloc(self, shape, dt):
        assert shape[0] == 128
        n = 1
        for s in shape[1:]:
            n *= s
        nb = n * (4 if dt == F32 else 2)
        nb_al = (nb + 63) // 64 * 64
        assert self.off + nb_al <= self.n, ("arena overflow", self.off, nb_al, self.n)
        ap = self.base[:, self.off // 2:(self.off + nb) // 2]
        self.off += nb_al
        if dt == F32:
            ap = ap.bitcast(F32)
        if len(shape) == 3:
            ap = ap.rearrange("p (a b) -> p a b", a=shape[1])
        elif len(shape) == 4:
            ap = ap.rearrange("p (a b c) -> p a b c", a=shape[1], b=shape[2])
        return ap


def build_nc():
    nc = bass.Bass("TRN2", target_bir_lowering=False)
    P = Prog()

    def din(name, shape, dt=F32):
        return nc.dram_tensor(name, list(shape), dt, kind="ExternalInput").ap()

    def dscr(name, shape, dt):
        if name in DEBUG_OUT:
            return nc.dram_tensor(name, list(shape), dt, kind="ExternalOutput").ap()
        return nc.dram_tensor(name, list(shape), dt).ap()

    x_d = din("x", [NT, D])
    w1g, w1u, w1d = din("w1g", [D, DFF]), din("w1u", [D, DFF]), din("w1d", [DFF, D])
    w2g, w2u, w2d = din("w2g", [D, DFF]), din("w2u", [D, DFF]), din("w2d", [DFF, D])
    win_d, wout_d = din("win", [D, DIN]), din("wout", [D, D])
    n1_d, nm_d, n2_d = din("n1", [1, D]), din("nm", [1, D]), din("n2", [1, D])
    gout_d = din("gout", [1, 256])
    gq_d, gk_d = din("gq", [128, 1]), din("gk", [128, 1])
    gup_d = din("gup", [16, 512])
    gbias_d = din("gbias", [128, 4])
    cos_d, sin_d = din("cosT", [128, NT]), din("sinT", [128, NT])
    m4_d = din("m4", [3, 128, 512])
    mgla_d = din("mgla", [128, 128])
    ident_d = din("ident", [128, 128])
    swap_d = din("swapT", [128, 128])
    flag_d = din("flag", [128, 1])
    out_d = nc.dram_tensor("out", [NT, D], F32, kind="ExternalOutput").ap()

    x1_d = dscr("x1s", [NT, D], F32)
    qgT_d = dscr("qgT", [512, NT], F32)
    kgT_d = dscr("kgT", [512, NT], F32)
    laT_d = dscr("laT", [512, NT], F32)
    vg_d = dscr("vgs", [NT, 1024], BF16)
    rg_d = dscr("rgs", [NT, 1024], BF16)
    qaT_d = dscr("qaT", [1024, NT], BF16)
    kaT_d = [dscr("kaT%d" % i, [512, NT], BF16) for i in range(2)]
    kaT_g = [dscr("kaTg%d" % i, [1024, NT], BF16) for i in range(2)]
    va_d = [dscr("vas%d" % i, [HP, 1024], BF16) for i in range(2)]
    va_g = [dscr("vag%d" % i, [2 * HP, 1024], BF16) for i in range(2)]
    st_d = dscr("sts", [512, 256], F32)
    st_g = dscr("stg", [1024, 256], F32)
    oT_d = dscr("oTs", [D, NT], BF16)

    import contextlib
    with contextlib.ExitStack() as st:
        def sb(name, shape, dt):
            return st.enter_context(nc.sbuf_tensor("sb_" + name, list(shape), dt))

        AB = 172 * 1024
        arena_t = sb("arena", [128, AB // 2], BF16)
        gbc = sb("gbc", [128, D], F32)
        ident = sb("ident", [128, 128], BF16)
        swapT = sb("swapT", [128, 128], BF16)
        ones = sb("ones", [128, 128], BF16)
        mgla = sb("mgla", [128, 128], BF16)
        m4 = sb("m4", [128, 3, 512], BF16)
        gup = sb("gup", [16, 512], F32)
        small = sb("small", [128, 64], F32)
        wgl = sb("wgl", [128, 16, 16], BF16)
        goutb = sb("goutb", [128, 256], F32)
        ps = [st.enter_context(nc.psum_tensor("ps%d" % j, [128, 512], F32)) for j in range(8)]

        negb = small[:, 0:4]
        gq, gk = small[:, 4:5], small[:, 5:6]
        eps_t, one_t, flag = small[:, 6:7], small[:, 7:8], small[:, 8:9]
        ss = [small[:, 10 + j:11 + j] for j in range(2)]
        ss2 = [small[:, 12 + j:13 + j] for j in range(2)]
        rs = [small[:, 14 + j:15 + j] for j in range(2)]
        bar_t = small[:, 16:17]
        gbias_t = small[:, 20:24]

        def barrier():
            P.barrier(lambda e: e.memset(bar_t, 0.0))

        A = Arena(arena_t, AB)
        xs = [A.alloc([128, D], F32) for _ in range(8)]
        hT = A.alloc([128, 16, HP], BF16)
        W256 = [A.alloc([128, 16, 256], BF16) for _ in range(4)]
        tmp_off = A.off
        Wd = [A.alloc([128, 2, D], BF16) for _ in range(2)]
        actT = [A.alloc([128, 2, HP], BF16) for _ in range(2)]
        sg = [A.alloc([128, 512], F32) for _ in range(2)]
        hbs = [A.alloc([128, D], BF16) for _ in range(2)]
        jk = A.alloc([128, D], BF16)
        assert A.off <= AB
        A5 = Arena(arena_t, AB)
        A5.off = tmp_off
        stg = [A5.alloc([128, 1024], BF16) for _ in range(16)]
        cs_sb = A5.alloc([128, 2, HP], F32)

        def xk(i):
            return [("xs", i, d) for d in range(4)]

        def hk(t):
            return [("hT", i) for i in range(4 * t, 4 * t + 4)]

        wcnt = [0]
        wnobar = [False]
        pscnt = [0]

        def next_ps():
            j = pscnt[0] % 8
            pscnt[0] += 1
            return j

        P.add("sp", lambda e: e.dma_start(out=gup[:], in_=gup_d), w=[("gup",)], dma=True)
        P.add("sp", lambda e: e.dma_start(out=gbias_t, in_=gbias_d), w=[("small", "gb")], dma=True)
        P.add("sp", lambda e: e.dma_start(out=gq, in_=gq_d), w=[("small", "gq")], dma=True)
        P.add("sp", lambda e: e.dma_start(out=gk, in_=gk_d), w=[("small", "gk")], dma=True)
        P.add("sp", lambda e: e.dma_start(out=flag, in_=flag_d), w=[("small", "flag")], dma=True)
        P.add("sp", lambda e: e.dma_start(out=goutb[:], in_=gout_d.broadcast_to([128, 256])),
              w=[("goutb",)], dma=True)
        P.add("pool", lambda e: e.dma_start(out=ident[:], in_=ident_d), w=[("ident",)], dma=True)
        P.add("pool", lambda e: e.dma_start(out=swapT[:], in_=swap_d), w=[("swapT",)], dma=True)
        P.add("pool", lambda e: e.dma_start(out=mgla[:], in_=mgla_d), w=[("mgla",)], dma=True)
        P.add("pool", lambda e: e.dma_start(out=m4[:], in_=m4_d.rearrange("a p f -> p a f")),
              w=[("m4",)], dma=True)
        P.add("pool", lambda e: e.dma_start(
            out=wgl[:], in_=win_d.rearrange("(k p) f -> p k f", p=128)[:, :, C_GL:C_GL + 16]),
            w=[("wgl",)], dma=True)
        P.add("dve", lambda e: e.memset(ones[:], 1.0), w=[("ones",)])
        P.add("dve", lambda e: e.memset(eps_t, EPS), w=[("small", "eps")])
        P.add("dve", lambda e: e.memset(one_t, 1.0), w=[("small", "one")])
        P.add("dve", lambda e: e.tensor_scalar(out=negb, in0=gbias_t, scalar1=-1.0, scalar2=None,
                                               op0=ALU.mult),
              r=[("small", "gb")], w=[("small", "negb")])

        def load_gain(g_d):
            P.add("sp", lambda e: e.dma_start(out=gbc[:], in_=g_d.broadcast_to([128, D])),
                  w=[("gbc",)], dma=True)

        def norm_stats(i):
            q = i % 2
            hb = hbs[q]
            P.add("dve", lambda e: e.memset(ss[q], 0.0), w=[("ss", q)])
            P.add("act", lambda e: e.activation(out=jk, in_=xs[i], func=AF.Square, accum_out=ss[q]),
                  r=xk(i), w=[("jk",), ("ss", q)])
            P.add("act", lambda e: e.activation(out=ss2[q], in_=ss[q], func=AF.Sqrt, bias=eps_t,
                                                scale=1.0 / D),
                  r=[("ss", q), ("small", "eps")], w=[("ss2", q)])
            P.add("dve", lambda e: e.reciprocal(out=rs[q], in_=ss2[q]), r=[("ss2", q)], w=[("rs", q)])
            P.add("dve", lambda e: e.scalar_tensor_tensor(out=hb, in0=xs[i], scalar=rs[q], in1=gbc[:],
                                                          op0=ALU.mult, op1=ALU.mult),
                  r=xk(i) + [("rs", q), ("gbc",)], w=[("hb", q)])

        def norm_tr(i):
            q = i % 2
            hb = hbs[q]
            for half in range(2):
                j = next_ps()
                pb = ps[j][:].bitcast(BF16)
                for kk in range(8):
                    k = half * 8 + kk
                    P.add("pe", lambda e, k=k, kk=kk, pb=pb: e.transpose(
                        out=pb[:, kk * 128:(kk + 1) * 128], in_=hb[:, k * 128:(k + 1) * 128],
                        identity=ident[:]),
                        r=[("hb", q), ("ident",)], w=[("ps", j)])
                src = pb.rearrange("p (k t) -> p k t", k=8)
                dst = hT[:, half * 8:(half + 1) * 8, i * 128:(i + 1) * 128]
                if half == 0:
                    P.add("act", lambda e, src=src, dst=dst: e.activation(out=dst, in_=src, func=AF.Copy),
                          r=[("ps", j)], w=[("hT", i)])
                else:
                    P.add("dve", lambda e, src=src, dst=dst: e.tensor_copy(out=dst, in_=src),
                          r=[("ps", j)], w=[("hT", i)])

        def norm_all(pre=None):
            for s_ in range(9):
                if s_ < 8:
                    if pre is not None:
                        pre(s_)
                    norm_stats(s_)
                if s_ >= 1:
                    norm_tr(s_ - 1)

        def load_w256(w_d, c0):
            b = wcnt[0] % 4
            wcnt[0] += 1
            P.add("pool", lambda e: e.dma_start(
                out=W256[b], in_=w_d.rearrange("(k p) f -> p k f", p=128)[:, :, c0:c0 + 256]),
                w=[("W", b)], dma=True, nobar=wnobar[0])
            return b

        def ffn(wg_d, wu_d, wd_d):
            NG = DFF // 256
            st_ = {}

            def gu(g):
                bg = load_w256(wg_d, g * 256)
                bu = load_w256(wu_d, g * 256)
                bd = g % 2
                P.add("pool", lambda e: e.dma_start(
                    out=Wd[bd], in_=wd_d[g * 256:(g + 1) * 256, :].rearrange("(c p) d -> p c d", p=128)),
                    w=[("Wd", bd)], dma=True)
                for c in range(2):
                    for t in range(2):
                        q = (2 * c + t) % 2
                        jg, ju = q, 2 + q
                        for k in range(16):
                            P.add("pe", lambda e, k=k, c=c, t=t, jg=jg: e.matmul(
                                ps[jg][:], lhsT=W256[bg][:, k, c * 128:(c + 1) * 128],
                                rhs=hT[:, k, t * 512:(t + 1) * 512], start=(k == 0), stop=(k == 15)),
                                r=[("W", bg)] + hk(t), w=[("ps", jg)])
                        for k in range(16):
                            P.add("pe", lambda e, k=k, c=c, t=t, ju=ju: e.matmul(
                                ps[ju][:], lhsT=W256[bu][:, k, c * 128:(c + 1) * 128],
                                rhs=hT[:, k, t * 512:(t + 1) * 512], start=(k == 0), stop=(k == 15)),
                                r=[("W", bu)] + hk(t), w=[("ps", ju)])
                        P.add("act", lambda e, jg=jg, q=q: e.activation(out=sg[q], in_=ps[jg][:], func=AF.Silu),
                              r=[("ps", jg)], w=[("sg", q)])
                        P.add("dve", lambda e, ju=ju, q=q, c=c, t=t: e.tensor_tensor(
                            out=actT[bd][:, c, t * 512:(t + 1) * 512], in0=ps[ju][:], in1=sg[q], op=ALU.mult),
                            r=[("ps", ju), ("sg", q)], w=[("actT", bd, c, t)])

            def down(g):
                bd = g % 2
                n = 0
                for i in range(8):
                    for dg in range(4):
                        j = 4 + n % 4
                        n += 1
                        for c in range(2):
                            P.add("pe", lambda e, c=c, i=i, dg=dg, j=j: e.matmul(
                                ps[j][:], lhsT=actT[bd][:, c, i * 128:(i + 1) * 128],
                                rhs=Wd[bd][:, c, dg * 512:(dg + 1) * 512], start=(c == 0), stop=(c == 1)),
                                r=[("actT", bd, c, i // 4), ("Wd", bd)], w=[("ps", j)])
                        P.add("dve", lambda e, i=i, dg=dg, j=j: e.scalar_tensor_tensor(
                            out=xs[i][:, dg * 512:(dg + 1) * 512], in0=ps[j][:], scalar=0.5,
                            in1=xs[i][:, dg * 512:(dg + 1) * 512], op0=ALU.mult, op1=ALU.add),
                            r=[("ps", j), ("xs", i, dg)], w=[("xs", i, dg)])

            gu(0)
            for g in range(1, NG):
                gu(g)
                down(g - 1)
            down(NG - 1)

        def phaseA(hp):
            T0 = hp * HP

            def ld_x(hp_):
                for i in range(8):
                    P.add("sp", lambda e, i=i: e.dma_start(
                        out=xs[i], in_=x_d[hp_ * HP + i * 128:hp_ * HP + (i + 1) * 128, :]),
                        w=xk(i), dma=True)
            if hp == 0:
                ld_x(0)
            load_gain(n1_d)
            norm_all()
            if "noffn" not in PHASES:
                ffn(w1g, w1u, w1d)
            load_gain(nm_d)
            def st_x1(i):
                P.add("sp", lambda e, i=i: e.dma_start(out=x1_d[T0 + i * 128:T0 + (i + 1) * 128, :], in_=xs[i]),
                      r=xk(i), w=[("x1_d", hp, i)], dma=True)
            norm_all(st_x1)
            barrier()
            if hp == 0 and "A1" in PHASES:
                ld_x(1)
            if "nowin" not in PHASES:
                win_phase(hp)
            barrier()

        scnt = [0]

        def next_stg():
            j = scnt[0] % 16
            scnt[0] += 1
            return j

        def win_phase(hp):
            T0 = hp * HP
            P.add("sp", lambda e: e.dma_start(out=cs_sb[:, 0, :], in_=cos_d[:, T0:T0 + HP]), w=[("cos",)], dma=True)
            P.add("sp", lambda e: e.dma_start(out=cs_sb[:, 1, :], in_=sin_d[:, T0:T0 + HP]), w=[("sin",)], dma=True)

            def fm_chunk(b, sub, t, extra_w=()):
                j = next_ps()
                for k in range(16):
                    P.add("pe", lambda e, k=k: e.matmul(
                        ps[j][:], lhsT=W256[b][:, k, sub * 128:(sub + 1) * 128],
                        rhs=hT[:, k, t * 512:(t + 1) * 512], start=(k == 0), stop=(k == 15)),
                        r=[("W", b)] + hk(t), w=[("ps", j)])
                return j

            for (c0, dst_d, nm) in (((C_QG, qgT_d, "qgT"), (C_KG, kgT_d, "kgT")) if "no_wq" not in PHASES else ()):
                for pair in range(2):
                    b = load_w256(win_d, c0 + pair * 256)
                    for sub in range(2):
                        hd = pair * 2 + sub
                        for t in range(2):
                            j = fm_chunk(b, sub, t)
                            s = next_stg()
                            sv = stg[s].bitcast(F32)
                            P.add("act", lambda e, j=j, sv=sv: e.activation(out=sv, in_=ps[j][:], func=AF.Copy),
                                  r=[("ps", j)], w=[("stg", s)])
                            P.add("sp", lambda e, sv=sv, hd=hd, t=t, dst_d=dst_d: e.dma_start(
                                out=dst_d[hd * 128:(hd + 1) * 128, T0 + t * 512:T0 + (t + 1) * 512], in_=sv),
                                r=[("stg", s)], w=[(nm, hd, hp, t)], dma=True)
            for t in (range(2) if "no_wg" not in PHASES else ()):
                j = next_ps()
                for k in range(16):
                    P.add("pe", lambda e, k=k, j=j, t=t: e.matmul(
                        ps[j][0:16, :], lhsT=wgl[:, k, :], rhs=hT[:, k, t * 512:(t + 1) * 512],
                        start=(k == 0), stop=(k == 15)),
                        r=[("wgl",)] + hk(t), w=[("ps", j)])
                s = next_stg()
                gl = stg[s].bitcast(F32)[0:16, :]
                P.add("act", lambda e, j=j, gl=gl: e.activation(out=gl, in_=ps[j][0:16, :], func=AF.Copy),
                      r=[("ps", j)], w=[("stg", s)])
                for hd in range(4):
                    j2 = next_ps()
                    P.add("pe", lambda e, j2=j2, hd=hd, gl=gl: e.matmul(
                        ps[j2][:], lhsT=gup[:, hd * 128:(hd + 1) * 128], rhs=gl, start=True, stop=True),
                        r=[("gup",), ("stg", s)], w=[("ps", j2)])
                    s2 = next_stg()
                    ev = stg[s2].bitcast(F32)
                    P.add("act", lambda e, j2=j2, ev=ev, hd=hd: e.activation(
                        out=ev, in_=ps[j2][:], func=AF.Exp, bias=negb[:, hd:hd + 1], scale=-1.0),
                        r=[("ps", j2), ("small", "negb")], w=[("stg", s2)])
                    P.add("act", lambda e, ev=ev: e.activation(out=ev, in_=ev, func=AF.Ln, bias=one_t, scale=1.0),
                          r=[("stg", s2), ("small", "one")], w=[("stg", s2)])
                    P.add("sp", lambda e, ev=ev, hd=hd, t=t: e.dma_start(
                        out=laT_d[hd * 128:(hd + 1) * 128, T0 + t * 512:T0 + (t + 1) * 512], in_=ev),
                        r=[("stg", s2)], w=[("laT", hd, hp, t)], dma=True)
            for (c0, dst_d, nm, gvec, gkey) in (((C_QA, qaT_d, "qaT", gq, "gq"), (C_KA, kaT_d, "kaT", gk, "gk")) if "no_wa" not in PHASES else ()):
                pend = []
                for pair in range(4):
                    b = load_w256(win_d, c0 + pair * 256)
                    for sub in range(2):
                        hd = pair * 2 + sub
                        for t in range(2):
                            j = fm_chunk(b, sub, t)
                            while pend:
                                pend.pop(0)()
                            s_sq, s_xg, s_rs, s_t1, s_t2 = [next_stg() for _ in range(5)]
                            sqb, xg = stg[s_sq][:, 0:512], stg[s_xg][:, 0:512]
                            rsv, t1, t2 = stg[s_rs].bitcast(F32), stg[s_t1].bitcast(F32), stg[s_t2].bitcast(F32)
                            P.add("act", lambda e, j=j, sqb=sqb: e.activation(out=sqb, in_=ps[j][:], func=AF.Square),
                                  r=[("ps", j)], w=[("stg", s_sq)])
                            P.add("act", lambda e, j=j, xg=xg, gvec=gvec: e.activation(
                                out=xg, in_=ps[j][:], func=AF.Identity, scale=gvec),
                                r=[("ps", j), ("small", gkey)], w=[("stg", s_xg)])
                            def post(j=j, hd=hd, t=t, sqb=sqb, xg=xg, rsv=rsv, t1=t1, t2=t2,
                                     s_sq=s_sq, s_xg=s_xg, s_rs=s_rs, s_t1=s_t1, s_t2=s_t2,
                                     dst_d=dst_d, nm=nm):
                                j1, j2 = next_ps(), next_ps()
                                P.add("pe", lambda e, j1=j1, sqb=sqb: e.matmul(ps[j1][:], lhsT=ones[:], rhs=sqb,
                                                                               start=True, stop=True),
                                      r=[("ones",), ("stg", s_sq)], w=[("ps", j1)])
                                P.add("pe", lambda e, j2=j2, xg=xg: e.matmul(ps[j2][:], lhsT=swapT[:], rhs=xg,
                                                                             start=True, stop=True),
                                      r=[("swapT",), ("stg", s_xg)], w=[("ps", j2)])
                                P.add("act", lambda e, j1=j1, rsv=rsv: e.activation(
                                    out=rsv, in_=ps[j1][:], func=AF.Ln, bias=eps_t, scale=1.0 / 128),
                                    r=[("ps", j1), ("small", "eps")], w=[("stg", s_rs)])
                                P.add("act", lambda e, rsv=rsv: e.activation(out=rsv, in_=rsv, func=AF.Exp, scale=-0.5),
                                      r=[("stg", s_rs)], w=[("stg", s_rs)])
                                P.add("dve", lambda e, t1=t1, xg=xg, t=t: e.tensor_tensor(
                                    out=t1, in0=xg, in1=cs_sb[:, 0, t * 512:(t + 1) * 512], op=ALU.mult),
                                    r=[("stg", s_xg), ("cos",)], w=[("stg", s_t1)])
                                P.add("dve", lambda e, t2=t2, j2=j2, t=t: e.tensor_tensor(
                                    out=t2, in0=ps[j2][:], in1=cs_sb[:, 1, t * 512:(t + 1) * 512], op=ALU.mult),
                                    r=[("ps", j2), ("sin",)], w=[("stg", s_t2)])
                                P.add("dve", lambda e, t1=t1, t2=t2: e.tensor_tensor(out=t1, in0=t1, in1=t2, op=ALU.add),
                                      r=[("stg", s_t1), ("stg", s_t2)], w=[("stg", s_t1)])
                                P.add("dve", lambda e, t1=t1, rsv=rsv, sqb=sqb: e.tensor_tensor(
                                    out=sqb, in0=t1, in1=rsv, op=ALU.mult),
                                    r=[("stg", s_t1), ("stg", s_rs)], w=[("stg", s_sq)])
                                dd = dst_d[hd * 128:(hd + 1) * 128, :] if nm == "qaT" else \
                                    dst_d[hd // 4][(hd % 4) * 128:(hd % 4 + 1) * 128, :]
                                P.add("sp", lambda e, sqb=sqb, t=t, dd=dd: e.dma_start(
                                    out=dd[:, T0 + t * 512:T0 + (t + 1) * 512], in_=sqb),
                                    r=[("stg", s_sq)], w=[(nm, hd, hp, t)], dma=True)
                            pend.append(post)
                while pend:
                    pend.pop(0)()
            for (c0, dst_d, nm, fn_) in (((C_VG, vg_d, "vg", AF.Copy), (C_RG, rg_d, "rg", AF.Silu),
                                         (C_VA, va_d, "va", AF.Copy)) if "no_wt" not in PHASES else ()):
                for q4 in range(4):
                    b = load_w256(win_d, c0 + q4 * 256)
                    for i in range(8):
                        j = next_ps()
                        for k in range(16):
                            P.add("pe", lambda e, k=k, j=j, i=i, b=b: e.matmul(
                                ps[j][:, 0:256], lhsT=hT[:, k, i * 128:(i + 1) * 128], rhs=W256[b][:, k, :],
                                start=(k == 0), stop=(k == 15)),
                                r=[("W", b), ("hT", i)], w=[("ps", j)])
                        s = next_stg()
                        sv = stg[s][:, 0:256]
                        P.add("act", lambda e, j=j, sv=sv, fn_=fn_: e.activation(out=sv, in_=ps[j][:, 0:256], func=fn_),
                              r=[("ps", j)], w=[("stg", s)])
                        dd = dst_d[hp][i * 128:(i + 1) * 128, :] if nm == "va" else \
                            dst_d[T0 + i * 128:T0 + (i + 1) * 128, :]
                        P.add("sp", lambda e, sv=sv, q4=q4, dd=dd: e.dma_start(
                            out=dd[:, q4 * 256:(q4 + 1) * 256], in_=sv),
                            r=[("stg", s)], w=[(nm, hp, i, q4)], dma=True)

        def all_keys(nm, *dims):
            import itertools
            return [(nm,) + t for t in itertools.product(*[range(d) for d in dims])]

        def phaseB():
            B = Arena(arena_t, AB)
            fac = []
            for hd in range(4):
                fac.append(dict(qq=B.alloc([128, NT], BF16), kk=B.alloc([128, NT], BF16),
                                qi=B.alloc([128, NT], BF16), kst=B.alloc([128, 16, 128], BF16),
                                dec=B.alloc([128, 16], F32)))
            Sf = B.alloc([128, 256], F32)
            Sb = B.alloc([128, 256], BF16)
            Sin = B.alloc([128, 256], F32)
            mark = B.off
            for hg in range(2):
                P.add("pool", lambda e, hg=hg: e.collective_compute(
                    "AllGather", ALU.bypass, replica_groups=RGROUPS,
                    ins=[kaT_d[hg].opt()], outs=[kaT_g[hg].opt()]),
                    r=all_keys("kaT", 8, 2, 2), w=[("kaT_g", hg)], kind="cc")
            for hv in range(2):
                P.add("pool", lambda e, hv=hv: e.collective_compute(
                    "AllGather", ALU.bypass, replica_groups=RGROUPS,
                    ins=[va_d[hv].opt()], outs=[va_g[hv].opt()]),
                    r=all_keys("va", 2, 8, 4), w=[("va_g", hv)], kind="cc")
            p1 = [dict(qf=B.alloc([128, NT], F32), kf=B.alloc([128, NT], F32), d1=B.alloc([128, NT], F32),
                       vsb=B.alloc([128, 16, 256], BF16)) for _ in range(2)]
            bb, d4 = B.alloc([128, NT], F32), B.alloc([128, NT], F32)
            exs = [B.alloc([128, NT], F32) for _ in range(2)]
            rmask = B.alloc([128, NT], F32)
            P.add("dve", lambda e: e.memset(rmask, 1.0), w=[("rmask",)])
            P.add("dve", lambda e: e.memset(rmask[:, 0:NT:128], 0.0), w=[("rmask",)])
            SC = 128 ** -0.5

            def p1_loads(hd):
                si = hd % 2
                S_ = p1[si]
                P.add("sp", lambda e: e.dma_start(out=S_["qf"], in_=qgT_d[hd * 128:(hd + 1) * 128, :]),
                      r=all_keys("qgT", 4, 2, 2), w=[("qf", si)], dma=True)
                P.add("sp", lambda e: e.dma_start(out=S_["kf"], in_=kgT_d[hd * 128:(hd + 1) * 128, :]),
                      r=all_keys("kgT", 4, 2, 2), w=[("kf", si)], dma=True)
                P.add("sp", lambda e: e.dma_start(out=S_["d1"], in_=laT_d[hd * 128:(hd + 1) * 128, :]),
                      r=all_keys("laT", 4, 2, 2), w=[("d1", si)], dma=True)
                P.add("sp", lambda e: e.dma_start(
                    out=S_["vsb"], in_=vg_d[:, hd * 256:(hd + 1) * 256].rearrange("(i p) c -> p i c", p=128)),
                    r=all_keys("vg", 2, 8, 4), w=[("vsb1", si)], dma=True)

            def p1_compute(hd):
                si = hd % 2
                S_ = p1[si]
                f = fac[hd]
                qf, kf, d1, vsb = S_["qf"], S_["kf"], S_["d1"], S_["vsb"]
                P.add("act", lambda e: e.activation(out=d1, in_=d1, func=AF.Identity, scale=-1.0 / 16.0),
                      r=[("d1", si)], w=[("d1", si)])
                P.add("dve", lambda e: e.tensor_tensor_scan(out=bb, data0=rmask, data1=d1, initial=0.0,
                                                            op0=ALU.mult, op1=ALU.add),
                      r=[("d1", si), ("rmask",)], w=[("bb",)])
                b3 = bb.rearrange("p (i t) -> p i t", i=16)
                d13 = d1.rearrange("p (i t) -> p i t", i=16)
                d43 = d4.rearrange("p (i t) -> p i t", i=16)
                bref = b3[:, :, 63:64].broadcast_to([128, 16, 128])
                blast = b3[:, :, 127:128].broadcast_to([128, 16, 128])
                P.add("dve", lambda e: e.tensor_tensor(out=d13, in0=b3, in1=bref, op=ALU.subtract),
                      r=[("bb",)], w=[("d1", si)])
                P.add("dve", lambda e: e.tensor_tensor(out=d43, in0=blast, in1=b3, op=ALU.subtract),
                      r=[("bb",)], w=[("d4",)])
                P.add("act", lambda e: e.activation(out=exs[0], in_=d1, func=AF.Exp), r=[("d1", si)], w=[("ex", 0)])
                P.add("act", lambda e: e.activation(out=exs[1], in_=d1, func=AF.Exp, scale=-1.0),
                      r=[("d1", si)], w=[("ex", 1)])
                P.add("dve", lambda e: e.scalar_tensor_tensor(out=f["qq"], in0=qf, scalar=SC, in1=exs[0],
                                                              op0=ALU.mult, op1=ALU.mult),
                      r=[("qf", si), ("ex", 0)], w=[("fac", hd, "qq")])
                P.add("dve", lambda e: e.tensor_tensor(out=f["kk"], in0=kf, in1=exs[1], op=ALU.mult),
                      r=[("kf", si), ("ex", 1)], w=[("fac", hd, "kk")])
                P.add("act", lambda e: e.activation(out=exs[0], in_=bb, func=AF.Exp), r=[("bb",)], w=[("ex", 0)])
                P.add("act", lambda e: e.activation(out=f["dec"], in_=b3[:, :, 127], func=AF.Exp),
                      r=[("bb",)], w=[("fac", hd, "dec")])
                P.add("act", lambda e: e.activation(out=exs[1], in_=d4, func=AF.Exp), r=[("d4",)], w=[("ex", 1)])
                P.add("dve", lambda e: e.scalar_tensor_tensor(out=f["qi"], in0=qf, scalar=SC, in1=exs[0],
                                                              op0=ALU.mult, op1=ALU.mult),
                      r=[("qf", si), ("ex", 0)], w=[("fac", hd, "qi")])
                ksb = d4.bitcast(BF16)[:, 0:NT]
                P.add("dve", lambda e: e.tensor_tensor(out=ksb, in0=kf, in1=exs[1], op=ALU.mult),
                      r=[("kf", si), ("ex", 1)], w=[("d4",)])
                for half in range(2):
                    j = next_ps()
                    pb = ps[j][:].bitcast(BF16)
                    for ii in range(8):
                        i = half * 8 + ii
                        P.add("pe", lambda e, i=i, ii=ii, pb=pb: e.transpose(
                            out=pb[:, ii * 128:(ii + 1) * 128], in_=ksb[:, i * 128:(i + 1) * 128],
                            identity=ident[:]),
                            r=[("d4",), ("ident",)], w=[("ps", j)])
                    P.add("act", lambda e, pb=pb, half=half: e.activation(
                        out=f["kst"][:, half * 8:(half + 1) * 8, :],
                        in_=pb.rearrange("p (i d) -> p i d", i=8), func=AF.Copy),
                        r=[("ps", j)], w=[("fac", hd, "kst")])
                P.add("dve", lambda e: e.memset(Sf, 0.0), w=[("Sf",)])
                for i in range(16):
                    j = next_ps()
                    P.add("pe", lambda e, j=j, i=i: e.matmul(ps[j][:, 0:256], lhsT=f["kst"][:, i, :],
                                                             rhs=vsb[:, i, :], start=True, stop=True),
                          r=[("fac", hd, "kst"), ("vsb1", si)], w=[("ps", j)])
                    P.add("dve", lambda e, j=j, i=i: e.scalar_tensor_tensor(
                        out=Sf, in0=Sf, scalar=f["dec"][:, i:i + 1], in1=ps[j][:, 0:256],
                        op0=ALU.mult, op1=ALU.add),
                        r=[("ps", j), ("Sf",), ("fac", hd, "dec")], w=[("Sf",)])
                P.add("sp", lambda e: e.dma_start(out=st_d[hd * 128:(hd + 1) * 128, :], in_=Sf),
                      r=[("Sf",)], w=[("st_d", hd)], dma=True)

            p1_loads(0)
            for hd in range(4):
                if hd + 1 < 4:
                    p1_loads(hd + 1)
                p1_compute(hd)
            if "noBx" in PHASES:
                return
            P.add("pool", lambda e: e.collective_compute(
                "AllGather", ALU.bypass, replica_groups=RGROUPS,
                ins=[st_d.opt()], outs=[st_g.opt()]),
                r=[("st_d", h_) for h_ in range(4)], w=[("st_g",)], kind="cc")
            barrier()
            if "noBa" in PHASES:
                return
            B.off = mark
            sets = []
            for s_ in range(2):
                sets.append(dict(
                    qT=B.alloc([128, NT], BF16), kT=B.alloc([128, NT], BF16), kTp=B.alloc([128, NT], BF16),
                    Vr={1: B.alloc([128, 16, 128], BF16), 4: B.alloc([128, 16, 128], BF16),
                        16: B.alloc([128, 16, 128], BF16)},
                    Vp={1: B.alloc([128, 1, 128], BF16), 4: B.alloc([128, 4, 128], BF16),
                        16: B.alloc([128, 16, 128], BF16)}))
            accs = [B.alloc([128, 2, NT], F32) for _ in range(2)]
            PT = [B.alloc([128, 512], BF16) for _ in range(3)]
            oTb = [B.alloc([128, NT], BF16) for _ in range(2)]
            SCA = 128 ** -0.5

            def att_loads(hd):
                si = hd % 2
                S_ = sets[si]
                P.add("sp", lambda e: e.dma_start(out=S_["qT"], in_=qaT_d[hd * 128:(hd + 1) * 128, :]),
                      r=all_keys("qaT", 8, 2, 2), w=[("qT", si)], dma=True)
                P.add("sp", lambda e: e.dma_start(
                    out=S_["kT"], in_=kaT_d[hd // 4][(hd % 4) * 128:(hd % 4 + 1) * 128, :]),
                    r=all_keys("kaT", 8, 2, 2), w=[("kT", si)], dma=True)
                P.add("sp", lambda e: e.dma_start(
                    out=S_["kTp"], in_=kaT_g[hd // 4][(hd % 4) * 128:(hd % 4 + 1) * 128, :]),
                    r=[("kaT_g", hd // 4)], w=[("kTp", si)], dma=True)
                hc = slice(hd * 128, (hd + 1) * 128)
                for r_ in (1, 4, 16):
                    NL = 16 // r_
                    for hv in range(2):
                        for (srcT, dstT, rk, wk, prev) in (
                                (va_d[hv], S_["Vr"][r_], all_keys("va", 2, 8, 4), ("Vr", r_, si), False),
                                (va_g[hv][0:HP, :], S_["Vp"][r_], [("va_g", hv)], ("Vp", r_, si), True)):
                            src = srcT[:, hc]
                            if r_ == 16:
                                src = src.rearrange("(j r) c -> j r c", r=16)
                                dst = dstT[hv * 64:(hv + 1) * 64, :, :]
                            elif r_ == 1:
                                src = src.rearrange("(n j) c -> j n c", j=128)
                                if prev:
                                    if hv == 0:
                                        continue
                                    src = src[:, 7:8, :]
                                    dst = dstT[:, 0:1, :]
                                else:
                                    dst = dstT[:, hv * 8:(hv + 1) * 8, :]
                            else:
                                NLh = NL // 2
                                src4 = src.rearrange("(n j r) c -> n j r c", j=128, r=r_)
                                dst4 = dstT if prev else dstT.rearrange("p (r n) c -> p r n c", r=r_)
                                if prev:
                                    if hv == 0:
                                        continue
                                    P.add("sp", lambda e, src=src4[NLh - 1], dst=dst4[:, :, :]: e.dma_start(
                                        out=dst, in_=src), r=rk, w=[wk], dma=True)
                                else:
                                    for n in range(NLh):
                                        P.add("sp", lambda e, src=src4[n], dst=dst4[:, :, hv * NLh + n, :]:
                                              e.dma_start(out=dst, in_=src), r=rk, w=[wk], dma=True)
                                continue
                            P.add("sp", lambda e, src=src, dst=dst: e.dma_start(out=dst, in_=src),
                                  r=rk, w=[wk], dma=True)

            def att_compute(hd):
                si = hd % 2
                S_ = sets[si]
                qT, kT, kTp, Vr, Vp = S_["qT"], S_["kT"], S_["kTp"], S_["Vr"], S_["Vp"]
                acc = accs[si]
                G = []
                for r_ in (1, 4, 16):
                    NL = 16 // r_
                    blocks = [(res, nl) for res in range(r_) for nl in range(NL)]
                    for g0 in range(0, 16, 2):
                        pair = blocks[g0:g0 + 2]
                        first = [nl == 0 for (_, nl) in pair]
                        mv = 1 if not (first[0] or first[1]) else (2 if (first[0] and first[1]) else 0)
                        sl = []
                        for (res, nl) in pair:
                            q0 = res + r_ * 128 * nl
                            qs = slice(q0, q0 + r_ * 127 + 1, r_)
                            if nl == 0:
                                p0 = res + r_ * 128 * (NL - 1)
                                kprev = kTp[:, p0:p0 + r_ * 127 + 1:r_]
                                vprev = Vp[r_][:, res, :]
                                kpk, vpk = ("kTp", si), ("Vp", r_, si)
                            else:
                                p0 = res + r_ * 128 * (nl - 1)
                                kprev = kT[:, p0:p0 + r_ * 127 + 1:r_]
                                vprev = Vr[r_][:, res * NL + nl - 1, :]
                                kpk, vpk = ("kT", si), ("Vr", r_, si)
                            vcur = Vr[r_][:, res * NL + nl, :]
                            sl.append((qs, kprev, kpk, vprev, vpk, vcur))
                        G.append(dict(r_=r_, mv=mv, sl=sl))
                NG = len(G)

                def bank(g):
                    gg = hd * NG + g
                    return gg % 3, 3 + gg % 3, PT[gg % 3]

                def S(g):
                    jS, jO, pt = bank(g)
                    mv = G[g]["mv"]
                    P.add("pe", lambda e: e.matmul(ps[jS][:], lhsT=ident[:], rhs=m4[:, mv, :],
                                                   start=True, stop=False),
                          r=[("ident",), ("m4",)], w=[("ps", jS)])
                    for bi, (qs, kprev, kpk, vprev, vpk, vcur) in enumerate(G[g]["sl"]):
                        last = (bi == 1)
                        P.add("pe", lambda e, bi=bi, kprev=kprev, qs=qs: e.matmul(
                            ps[jS][:, (2 * bi) * 128:(2 * bi + 1) * 128], lhsT=kprev, rhs=qT[:, qs],
                            start=False, stop=False),
                            r=[kpk, ("qT", si)], w=[("ps", jS)])
                        P.add("pe", lambda e, bi=bi, qs=qs, last=last: e.matmul(
                            ps[jS][:, (2 * bi + 1) * 128:(2 * bi + 2) * 128], lhsT=kT[:, qs], rhs=qT[:, qs],
                            start=False, stop=last),
                            r=[("kT", si), ("qT", si)], w=[("ps", jS)])

                def E(g):
                    jS, jO, pt = bank(g)
                    P.add("act", lambda e: e.activation(out=pt, in_=ps[jS][:], func=AF.Exp, scale=SCA),
                          r=[("ps", jS)], w=[("PT", jS)])

                def V(g):
                    jS, jO, pt = bank(g)
                    r_ = G[g]["r_"]
                    po = ps[jO][:].rearrange("p (b a q) -> p b a q", b=2, a=2)
                    for bi, (qs, kprev, kpk, vprev, vpk, vcur) in enumerate(G[g]["sl"]):
                        P.add("pe", lambda e, bi=bi, vprev=vprev: e.matmul(
                            po[:, bi, 0, :], lhsT=vprev, rhs=pt[:, (2 * bi) * 128:(2 * bi + 1) * 128],
                            start=True, stop=False),
                            r=[vpk, ("PT", jS)], w=[("ps", jO)])
                        P.add("pe", lambda e, bi=bi, vcur=vcur: e.matmul(
                            po[:, bi, 0, :], lhsT=vcur, rhs=pt[:, (2 * bi + 1) * 128:(2 * bi + 2) * 128],
                            start=False, stop=True),
                            r=[("Vr", r_, si), ("PT", jS)], w=[("ps", jO)])
                        P.add("pe", lambda e, bi=bi: e.matmul(
                            po[:, bi, 1, :], lhsT=ones[:], rhs=pt[:, (2 * bi) * 128:(2 * bi + 1) * 128],
                            start=True, stop=False),
                            r=[("ones",), ("PT", jS)], w=[("ps", jO)])
                        P.add("pe", lambda e, bi=bi: e.matmul(
                            po[:, bi, 1, :], lhsT=ones[:], rhs=pt[:, (2 * bi + 1) * 128:(2 * bi + 2) * 128],
                            start=False, stop=True),
                            r=[("ones",), ("PT", jS)], w=[("ps", jO)])

                def Dv(g):
                    jS, jO, pt = bank(g)
                    r_ = G[g]["r_"]
                    po = ps[jO][:].rearrange("p (b a q) -> p b a q", b=2, a=2)
                    for bi, (qs, kprev, kpk, vprev, vpk, vcur) in enumerate(G[g]["sl"]):
                        if r_ == 1:
                            P.add("act", lambda e, bi=bi, qs=qs: e.activation(
                                out=acc[:, :, qs], in_=po[:, bi, :, :], func=AF.Copy),
                                r=[("ps", jO)], w=[("acc", si)])
                        else:
                            P.add("dve", lambda e, bi=bi, qs=qs: e.tensor_tensor(
                                out=acc[:, :, qs], in0=po[:, bi, :, :], in1=acc[:, :, qs], op=ALU.add),
                                r=[("ps", jO), ("acc", si)], w=[("acc", si)])

                S(0)
                S(1)
                for g in range(NG):
                    E(g)
                    V(g)
                    if g + 2 < NG:
                        S(g + 2)
                    Dv(g)
                ob_ = oTb[si]
                P.add("dve", lambda e: e.reciprocal(out=acc[:, 1, :], in_=acc[:, 1, :]),
                      r=[("acc", si)], w=[("acc", si)])
                P.add("dve", lambda e: e.tensor_tensor(out=ob_, in0=acc[:, 0, :], in1=acc[:, 1, :], op=ALU.mult),
                      r=[("acc", si)], w=[("oTb", si)])
                P.add("sp", lambda e: e.dma_start(out=oT_d[1024 + hd * 128:1024 + (hd + 1) * 128, :], in_=ob_),
                      r=[("oTb", si)], w=[("oT_d", 8 + hd)], dma=True)

            att_loads(0)
            for hd in range(8):
                if hd + 1 < 8:
                    att_loads(hd + 1)
                att_compute(hd)
            barrier()
            if "noB2" in PHASES:
                return
            B.off = mark
            g2 = []
            for s_ in range(2):
                g2.append(dict(vsb=B.alloc([128, 16, 256], BF16), rsb=B.alloc([128, 16, 256], BF16),
                               Sball=B.alloc([128, 16, 256], BF16), oTg=B.alloc([128, 2, NT], BF16),
                               sc=B.alloc([128, 48], F32), Sin=B.alloc([128, 256], F32)))
            ot = [B.alloc([128, 256], F32) for _ in range(4)]
            ob = [B.alloc([128, 256], BF16) for _ in range(4)]
            Sm = [B.alloc([128, 128], BF16) for _ in range(2)]
            junk = B.alloc([128, 256], BF16)

            def g2_loads(hd):
                si = hd % 2
                S_ = g2[si]
                P.add("sp", lambda e: e.dma_start(
                    out=S_["vsb"], in_=vg_d[:, hd * 256:(hd + 1) * 256].rearrange("(i p) c -> p i c", p=128)),
                    r=all_keys("vg", 2, 8, 4), w=[("vsb", si)], dma=True)
                P.add("sp", lambda e: e.dma_start(
                    out=S_["rsb"], in_=rg_d[:, hd * 256:(hd + 1) * 256].rearrange("(i p) c -> p i c", p=128)),
                    r=all_keys("rg", 2, 8, 4), w=[("rsb", si)], dma=True)
                P.add("sp", lambda e: e.dma_start(out=S_["Sin"], in_=st_g[hd * 128:(hd + 1) * 128, :]),
                      r=[("st_g",)], w=[("Sin", si)], dma=True)

            def g2_compute(hd):
                si = hd % 2
                S_ = g2[si]
                f = fac[hd]
                vsb_, rsb_, Sball, oTg, sc = S_["vsb"], S_["rsb"], S_["Sball"], S_["oTg"], S_["sc"]
                P.add("dve", lambda e: e.tensor_scalar(out=Sf, in0=S_["Sin"], scalar1=flag, scalar2=None,
                                                       op0=ALU.mult),
                      r=[("Sin", si), ("small", "flag")], w=[("Sf",)])
                P.add("act", lambda e: e.activation(out=Sball[:, 0, :], in_=Sf, func=AF.Copy),
                      r=[("Sf",)], w=[("Sball", si, 0)])
                P.add("dve", lambda e: e.memset(sc[:, 0:16], 0.0), w=[("sc", si, "ss")])
                for i in range(15):
                    jU = 6 + i % 2
                    P.add("pe", lambda e, jU=jU, i=i: e.matmul(ps[jU][:, 0:256], lhsT=f["kst"][:, i, :],
                                                               rhs=vsb_[:, i, :], start=True, stop=True),
                          r=[("fac", hd, "kst"), ("vsb", si)], w=[("ps", jU)])
                    P.add("dve", lambda e, jU=jU, i=i: e.scalar_tensor_tensor(
                        out=Sf, in0=Sf, scalar=f["dec"][:, i:i + 1], in1=ps[jU][:, 0:256],
                        op0=ALU.mult, op1=ALU.add),
                        r=[("ps", jU), ("Sf",), ("fac", hd, "dec")], w=[("Sf",)])
                    P.add("act", lambda e, i=i: e.activation(out=Sball[:, i + 1, :], in_=Sf, func=AF.Copy),
                          r=[("Sf",)], w=[("Sball", si, i + 1)])

                def St(i):
                    tk = slice(i * 128, (i + 1) * 128)
                    jA = i % 2
                    sm = Sm[i % 2]
                    P.add("pe", lambda e: e.matmul(ps[jA][:, 0:128], lhsT=f["kk"][:, tk], rhs=f["qq"][:, tk],
                                                   start=True, stop=True),
                          r=[("fac", hd, "kk"), ("fac", hd, "qq")], w=[("ps", jA)])
                    P.add("dve", lambda e: e.tensor_tensor(out=sm, in0=ps[jA][:, 0:128], in1=mgla[:], op=ALU.mult),
                          r=[("ps", jA), ("mgla",)], w=[("Sm", i % 2)])

                def Oc(i):
                    tk = slice(i * 128, (i + 1) * 128)
                    jO_ = 2 + i % 3
                    sm = Sm[i % 2]
                    q4 = i % 4
                    P.add("pe", lambda e: e.matmul(ps[jO_][:, 0:256], lhsT=sm, rhs=vsb_[:, i, :],
                                                   start=True, stop=False),
                          r=[("Sm", i % 2), ("vsb", si)], w=[("ps", jO_)])
                    P.add("pe", lambda e: e.matmul(ps[jO_][:, 0:256], lhsT=f["qi"][:, tk], rhs=Sball[:, i, :],
                                                   start=False, stop=True),
                          r=[("fac", hd, "qi"), ("Sball", si, i)], w=[("ps", jO_)])
                    P.add("act", lambda e: e.activation(out=junk, in_=ps[jO_][:, 0:256], func=AF.Square,
                                                        accum_out=sc[:, i:i + 1]),
                          r=[("ps", jO_), ("sc", si, "ss")], w=[("junk",), ("sc", si, "ss", i)])
                    P.add("act", lambda e: e.activation(out=sc[:, 16 + i:17 + i], in_=sc[:, i:i + 1], func=AF.Sqrt,
                                                        bias=eps_t, scale=1.0 / 256),
                          r=[("sc", si, "ss", i), ("small", "eps")], w=[("sc", si, "ss2", i)])
                    P.add("dve", lambda e: e.reciprocal(out=sc[:, 32 + i:33 + i], in_=sc[:, 16 + i:17 + i]),
                          r=[("sc", si, "ss2", i)], w=[("sc", si, "rs", i)])
                    P.add("dve", lambda e: e.scalar_tensor_tensor(out=ot[q4], in0=ps[jO_][:, 0:256],
                                                                  scalar=sc[:, 32 + i:33 + i], in1=goutb[:],
                                                                  op0=ALU.mult, op1=ALU.mult),
                          r=[("ps", jO_), ("sc", si, "rs", i), ("goutb",)], w=[("ot", q4)])
                    P.add("dve", lambda e: e.tensor_tensor(out=ob[q4], in0=ot[q4], in1=rsb_[:, i, :], op=ALU.mult),
                          r=[("ot", q4), ("rsb", si)], w=[("ob", q4)])

                def Tr(i):
                    tk = slice(i * 128, (i + 1) * 128)
                    q4 = i % 4
                    pb = ps[5][:].bitcast(BF16)
                    for c in range(2):
                        P.add("pe", lambda e, c=c: e.transpose(out=pb[:, c * 128:(c + 1) * 128],
                                                               in_=ob[q4][:, c * 128:(c + 1) * 128],
                                                               identity=ident[:]),
                              r=[("ob", q4), ("ident",)], w=[("ps", 5)])
                    P.add("act", lambda e: e.activation(
                        out=oTg[:, :, tk], in_=pb[:, 0:256].rearrange("p (c t) -> p c t", c=2), func=AF.Copy),
                        r=[("ps", 5)], w=[("oTg", si)])

                for s_ in range(16 + 3):
                    if s_ < 16:
                        St(s_)
                    if 0 <= s_ - 1 < 16:
                        Oc(s_ - 1)
                    if 0 <= s_ - 3 < 16:
                        Tr(s_ - 3)
                for c in range(2):
                    P.add("sp", lambda e, c=c: e.dma_start(
                        out=oT_d[hd * 256 + c * 128:hd * 256 + (c + 1) * 128, :], in_=oTg[:, c, :]),
                        r=[("oTg", si)], w=[("oT_d", hd * 2 + c)], dma=True)

            g2_loads(0)
            for hd in range(4):
                if hd + 1 < 4:
                    g2_loads(hd + 1)
                g2_compute(hd)
            barrier()

        def phaseC(hp):
            T0 = hp * HP

            def ld_oT(hp_):
                for k4 in range(4):
                    P.add("sp", lambda e, k4=k4: e.dma_start(
                        out=hT[:, k4 * 4:(k4 + 1) * 4, :],
                        in_=oT_d[k4 * 512:(k4 + 1) * 512, hp_ * HP:(hp_ + 1) * HP].rearrange("(k p) t -> p k t", p=128)),
                        r=[("oT_d", j) for j in range(16)], w=[("hT", i) for i in range(8)], dma=True)

            def ld_x1(hp_, i):
                P.add("sp", lambda e: e.dma_start(
                    out=xs[i], in_=x1_d[hp_ * HP + i * 128:hp_ * HP + (i + 1) * 128, :]),
                    r=[("x1_d", hp_, i)], w=xk(i), dma=True)

            if hp == 0 or "C0" not in PHASES:
                ld_oT(hp)
                for i in range(8):
                    ld_x1(hp, i)
            for dgp in range(8):
                b = load_w256(wout_d, dgp * 256)
                for i in range(8):
                    j = next_ps()
                    for k in range(16):
                        P.add("pe", lambda e, k=k, j=j, i=i, b=b: e.matmul(
                            ps[j][:, 0:256], lhsT=hT[:, k, i * 128:(i + 1) * 128], rhs=W256[b][:, k, :],
                            start=(k == 0), stop=(k == 15)),
                            r=[("W", b), ("hT", i)], w=[("ps", j)])
                    dgk = dgp // 2
                    P.add("dve", lambda e, j=j, i=i, dgp=dgp: e.tensor_tensor(
                        out=xs[i][:, dgp * 256:(dgp + 1) * 256], in0=ps[j][:, 0:256],
                        in1=xs[i][:, dgp * 256:(dgp + 1) * 256], op=ALU.add),
                        r=[("ps", j), ("xs", i, dgk)], w=[("xs", i, dgk)])
            load_gain(n2_d)
            norm_all()
            ffn(w2g, w2u, w2d)
            nxt = (hp == 0 and "C1" in PHASES)
            if nxt:
                ld_oT(1)
            for i in range(8):
                P.add("sp", lambda e, i=i: e.dma_start(out=out_d[T0 + i * 128:T0 + (i + 1) * 128, :], in_=xs[i]),
                      r=xk(i), w=[("out", hp, i)], dma=True)
                if nxt:
                    ld_x1(1, i)

        wnobar[0] = True
        if "A0" in PHASES:
            phaseA(0)
        if "A1" in PHASES:
            phaseA(1)
        wnobar[0] = False
        if "B" in PHASES:
            phaseB()
        if "C0" in PHASES:
            phaseC(0)
        if "C1" in PHASES:
            phaseC(1)

        P.schedule()
        P.emit(nc)
    return nc


_NC_CACHE = {}


def _host_consts(h):
    half = 64
    inv = 1.0 / (10000.0 ** (np.arange(half, dtype=np.float32) / half))
    pos = (h * NT + np.arange(NT)).astype(np.float32)
    ang = pos[None, :] * inv[:, None]
    cosT = np.concatenate([np.cos(ang), np.cos(ang)], 0).astype(np.float32)
    sinT = np.concatenate([-np.sin(ang), np.sin(ang)], 0).astype(np.float32)
    j = np.arange(128)[:, None]
    i = np.arange(128)[None, :]
    mcur = np.where(j <= i, 0.0, NEG).astype(np.float32)
    mprev = np.where(j >= i, 0.0, NEG).astype(np.float32)
    mprevh = mprev if h == 1 else np.full((128, 128), NEG, np.float32)
    m4 = np.stack([np.concatenate([mprevh, mcur, mprev, mcur], 1),
                   np.concatenate([mprev, mcur, mprev, mcur], 1),
                   np.concatenate([mprevh, mcur, mprevh, mcur], 1)], 0).astype(np.float32)
    mgla = (j <= i).astype(np.float32)
    ident = np.eye(128, dtype=np.float32)
    swapT = np.zeros((128, 128), np.float32)
    swapT[(np.arange(128) + 64) % 128, np.arange(128)] = 1.0
    flag = np.full((128, 1), float(h), np.float32)
    return dict(cosT=cosT, sinT=sinT, m4=m4, mgla=mgla, ident=ident, swapT=swapT, flag=flag)


def kernel(x, ffn1_norm, ffn1_w_gate, ffn1_w_up, ffn1_w_down, mix_norm, w_in,
           gla_gate_up, gla_gate_bias, gla_out_norm, att_q_norm, att_k_norm, w_out,
           ffn2_norm, ffn2_w_gate, ffn2_w_up, ffn2_w_down):
    f = lambda a: np.ascontiguousarray(np.asarray(a, dtype=np.float32))
    x = f(x)
    common = dict(
        w1g=f(ffn1_w_gate[0]), w1u=f(ffn1_w_up[0]), w1d=f(ffn1_w_down[0]),
        w2g=f(ffn2_w_gate[0]), w2u=f(ffn2_w_up[0]), w2d=f(ffn2_w_down[0]),
        win=f(w_in[0]), wout=f(w_out[0]),
        n1=f(ffn1_norm[0]).reshape(1, D), nm=f(mix_norm[0]).reshape(1, D), n2=f(ffn2_norm[0]).reshape(1, D),
        gout=f(gla_out_norm[0]).reshape(1, 256),
        gq=f(att_q_norm[0]).reshape(128, 1), gk=f(att_k_norm[0]).reshape(128, 1),
        gup=f(gla_gate_up[0]),
        gbias=f(np.asarray(gla_gate_bias[0]).reshape(4, 128).T),
    )
    consts = [_host_consts(0), _host_consts(1)]
    in_maps = []
    for c in range(8):
        b, h = c // 2, c % 2
        m = dict(common)
        m["x"] = np.ascontiguousarray(x[b, h * NT:(h + 1) * NT, :])
        m.update(consts[h])
        in_maps.append(m)
    if "nc" not in _NC_CACHE:
        _NC_CACHE["nc"] = build_nc()
    nc = _NC_CACHE["nc"]
    res = run_bass_kernel_spmd(nc, in_maps, core_ids=list(range(8)))
    out = np.empty((4, 4096, D), np.float32)
    for c in range(8):
        b, h = c // 2, c % 2
        out[b, h * NT:(h + 1) * NT, :] = res.results[c]["out"]
    return out
```
